# Optimizing a Trainium2 kernel written in Bass

```python
import jax, jax.numpy as jnp
from jax import lax
import numpy as np

D_MODEL = 4096
BATCH = 4
SEQ = 2048
DEPTH = 1
DEC_BATCH = 128
DEC_SEQ = 1
PAST_LEN = 16384
PAGE_SIZE = 128

E_A = D_MODEL // 2
CHUNK_A = 128
A_GROUPS = 16
A_GROUP_DIM = E_A // A_GROUPS
E_B = D_MODEL // 2
B_HEAD_DIM = 128
B_HEADS = E_B // B_HEAD_DIM
CHUNK_B = 64
COL_U = 0
COL_V = COL_U + E_A
COL_ZA = COL_V + E_A
COL_Q = COL_ZA + E_A
COL_F = COL_Q + E_B
COL_I = COL_F + E_B
COL_ZB = COL_I + E_B
COL_GA = COL_ZB + E_B
COL_GB = COL_GA + D_MODEL
N_COLS = COL_GB + D_MODEL
EPS = 1e-6

kernel_name = 'hybrid_gmlp_hgrn2_step'


def _rmsnorm(x, g):
    xf = x.astype(jnp.float32)
    y = xf * lax.rsqrt(jnp.mean(xf * xf, axis=-1, keepdims=True) + EPS)
    return (y * g.astype(jnp.float32)).astype(x.dtype)


def _layernorm(x, g, b):
    xf = x.astype(jnp.float32)
    mu = jnp.mean(xf, axis=-1, keepdims=True)
    xc = xf - mu
    y = xc * lax.rsqrt(jnp.mean(xc * xc, axis=-1, keepdims=True) + EPS)
    return (y * g.astype(jnp.float32) + b.astype(jnp.float32)).astype(x.dtype)


def _chunk_spatial_gate(v, w_s, b_s):
    bn, t, _ = v.shape
    pad = (-t) % CHUNK_A
    vp = jnp.pad(v, ((0, 0), (0, pad), (0, 0)))
    n = (t + pad) // CHUNK_A
    vc = vp.reshape(bn, n, CHUNK_A, A_GROUPS, A_GROUP_DIM)
    mask = jnp.tril(jnp.ones((CHUNK_A, CHUNK_A), dtype=bool))
    wm = jnp.where(mask[None], w_s, jnp.zeros((), w_s.dtype))
    mixed = jnp.einsum('gts,bnsgc->bntgc', wm, vc) + b_s.T[None, None, :, :, None]
    return mixed.reshape(bn, n * CHUNK_A, E_A)[:, :t]


def _hgrn2(q, log_f, k, i, s0):
    bn, t, h, _ = q.shape
    c = CHUNK_B if t >= CHUNK_B else t
    pad = (-t) % c
    if pad:
        pw = ((0, 0), (0, pad), (0, 0), (0, 0))
        q, log_f, k, i = [jnp.pad(a, pw) for a in (q, log_f, k, i)]
    n = (t + pad) // c

    def to_chunks(a):
        return a.reshape(bn, n, c, h, a.shape[-1]).transpose(1, 0, 2, 3, 4)

    mask = jnp.tril(jnp.ones((c, c), dtype=bool))[None, :, :, None, None]

    def step(s, inp):
        qc, lfc, kc, ic = inp
        b = jnp.cumsum(lfc, axis=1)
        o_inter = jnp.einsum('bthd,bhde->bthe', qc * jnp.exp(b), s)
        dec = jnp.exp(jnp.where(mask, b[:, :, None] - b[:, None, :], -jnp.inf))
        att = jnp.einsum('bthd,bshd,btshd->bhts', qc, kc, dec)
        o_intra = jnp.einsum('bhts,bshe->bthe', att, ic)
        b_last = b[:, -1]
        s_new = jnp.exp(b_last)[..., None] * s + jnp.einsum(
            'bshd,bshe->bhde', kc * jnp.exp(b_last[:, None] - b), ic)
        return s_new, o_inter + o_intra

    s_t, o = lax.scan(step, s0, (to_chunks(q), to_chunks(log_f), to_chunks(k), to_chunks(i)))
    o = o.transpose(1, 0, 2, 3, 4).reshape(bn, n * c, h, i.shape[-1])[:, :t]
    return o, s_t


def _layer(x, s0, layer_idx, lb_logits, g_pre, w_in, ln_g, ln_b, w_s, b_s,
           g_onorm, w_pa, w_pb, w_o, g_post):
    bn, t, _ = x.shape
    xn = _rmsnorm(x, g_pre)
    hcat = xn @ w_in
    hu = hcat[..., COL_U:COL_V]
    hv = hcat[..., COL_V:COL_ZA]
    za = hcat[..., COL_ZA:COL_Q]
    hq = hcat[..., COL_Q:COL_F]
    hf = hcat[..., COL_F:COL_I]
    hi = hcat[..., COL_I:COL_ZB]
    zb = hcat[..., COL_ZB:COL_GA]
    hga = hcat[..., COL_GA:COL_GB]
    hgb = hcat[..., COL_GB:N_COLS]

    u = jax.nn.gelu(hu)
    vn = _layernorm(jax.nn.gelu(hv), ln_g, ln_b)
    y_a = u * _chunk_spatial_gate(vn, w_s, b_s) * jax.nn.silu(za)

    lb = jnp.cumsum(jax.nn.softmax(lb_logits.astype(jnp.float32), axis=0), axis=0)[layer_idx]
    zf = hf.astype(jnp.float32)
    log_f = jnp.log(lb + (1.0 - lb) * jax.nn.sigmoid(zf))
    k = (1.0 - lb) * jax.nn.sigmoid(-zf)
    q = jax.nn.silu(hq.astype(jnp.float32))

    def heads(a):
        return a.reshape(bn, t, B_HEADS, B_HEAD_DIM)

    o, s_t = _hgrn2(heads(q), heads(log_f), heads(k), heads(hi.astype(jnp.float32)),
                    s0.astype(jnp.float32))
    o = o * lax.rsqrt(jnp.mean(o * o, axis=-1, keepdims=True) + EPS)
    o = o.reshape(bn, t, E_B) * g_onorm.astype(jnp.float32)
    y_b = o.astype(x.dtype) * jax.nn.silu(zb)

    merged = jax.nn.sigmoid(hga) * (y_a @ w_pa) + jax.nn.sigmoid(hgb) * (y_b @ w_pb)
    out = x + _rmsnorm(merged @ w_o, g_post)
    n_last = (t - 1) % CHUNK_A + 1
    return out, s_t.astype(x.dtype), vn[:, t - n_last:]


def setup_inputs(seed: int = 0) -> dict:
    key = jax.random.key(seed)
    ks = jax.random.split(key, 20)
    f32 = jnp.float32
    nrm = lambda k, shape, s: jax.random.normal(k, shape, f32) * s
    return {
        'x_prompt': nrm(ks[0], (BATCH, SEQ, D_MODEL), 1.0),
        'x_sample': nrm(ks[1], (DEC_BATCH, DEC_SEQ, D_MODEL), 1.0),
        'state_hgrn': nrm(ks[2], (DEPTH, DEC_BATCH, B_HEADS, B_HEAD_DIM, B_HEAD_DIM), 0.5),
        'lb_logits': nrm(ks[3], (DEPTH + 1, E_B), 1.0),
        'g_pre': 1.0 + nrm(ks[4], (DEPTH, D_MODEL), 0.05),
        'w_in': nrm(ks[5], (DEPTH, D_MODEL, N_COLS), D_MODEL ** -0.5),
        'ln_g': 1.0 + nrm(ks[6], (DEPTH, E_A), 0.05),
        'ln_b': nrm(ks[7], (DEPTH, E_A), 0.02),
        'w_s': nrm(ks[8], (DEPTH, A_GROUPS, CHUNK_A, CHUNK_A), CHUNK_A ** -0.5),
        'b_s': 1.0 + nrm(ks[9], (DEPTH, A_GROUPS, CHUNK_A), 0.1),
        'g_onorm': 1.0 + nrm(ks[10], (DEPTH, E_B), 0.05),
        'w_pa': nrm(ks[11], (DEPTH, E_A, D_MODEL), E_A ** -0.5),
        'w_pb': nrm(ks[12], (DEPTH, E_B, D_MODEL), E_B ** -0.5),
        'w_o': nrm(ks[13], (DEPTH, D_MODEL, D_MODEL), D_MODEL ** -0.5),
        'g_post': 1.0 + nrm(ks[14], (DEPTH, D_MODEL), 0.05),
    }


def reference(x_prompt, x_sample, state_hgrn, lb_logits, g_pre, w_in, ln_g, ln_b, w_s, b_s,
              g_onorm, w_pa, w_pb, w_o, g_post):
    yp, ys = x_prompt, x_sample
    sp_list, ss_list, vp_list, vs_list = [], [], [], []
    s0_prompt = jnp.zeros((x_prompt.shape[0], B_HEADS, B_HEAD_DIM, B_HEAD_DIM), x_prompt.dtype)
    for l in range(DEPTH):
        params = (g_pre[l], w_in[l], ln_g[l], ln_b[l], w_s[l], b_s[l],
                  g_onorm[l], w_pa[l], w_pb[l], w_o[l], g_post[l])
        yp, sp, vp = _layer(yp, s0_prompt, l, lb_logits, *params)
        ys, ss, vs = _layer(ys, state_hgrn[l], l, lb_logits, *params)
        sp_list.append(sp)
        ss_list.append(ss)
        vp_list.append(vp)
        vs_list.append(vs)
    state_hgrn_prompt = jnp.stack(sp_list)
    state_hgrn_sample = jnp.stack(ss_list)
    vrows_prompt = jnp.stack(vp_list)
    vrows_sample = jnp.stack(vs_list)
    return (yp, ys, state_hgrn_prompt, state_hgrn_sample, vrows_prompt, vrows_sample)
```

```python
import numpy as np
import concourse.bass as bass
import concourse.mybir as mybir
from concourse.bass_utils import run_bass_kernel_spmd

F32 = mybir.dt.float32
BF16 = mybir.dt.bfloat16
AF = mybir.ActivationFunctionType
ALU = mybir.AluOpType
EPS = 1e-6
GELU_C = 0.044715
GELU_S = 1.5957691216057308


class Cfg:
    def __init__(self, D=4096, NP=512, NS=8, NPASS=2, NCORES=8):
        self.D = D
        self.KC = D // 128
        self.E = D // 2
        self.H = self.E // 128
        self.NP, self.NS, self.NPASS, self.NCORES = NP, NS, NPASS, NCORES
        self.NT = NP + NS
        self.HW = self.NT // 2
        self.NB = NP // 128
        self.NCH = NP // 64
        self.NCOLS = 5 * D + D // 2
        E = self.E
        self.cU, self.cV, self.cZA, self.cQ, self.cF, self.cI, self.cZB = [i * E for i in range(7)]
        self.cGA = 7 * E
        self.cGB = 7 * E + D
        self.WB = 256


class Tile:
    __slots__ = ("name", "w", "r", "also", "excl")

    def __init__(self, name, also=(), excl=False):
        self.name = name
        self.excl = excl
        self.w = None
        self.r = {}
        self.also = list(also)


class Prog:
    ENG = ["pe", "act", "dve", "pool", "sp"]

    def __init__(self):
        self.ops = {e: [] for e in self.ENG}
        self.cnt = {e: 0 for e in self.ENG}
        self.known = {e: {} for e in self.ENG}
        self.lanes = {}
        self.dry = False
        self.tag = ""

    def _waits(self, eng, reads, writes):
        need = {}

        def add(tok):
            if tok is None:
                return
            s, v = tok
            if need.get(s, 0) < v:
                need[s] = v

        for t in reads:
            add(t.w)
        for t in writes:
            add(t.w)
            for s, v in t.r.items():
                add((s, v))
        out = []
        kn = self.known[eng]
        for s, v in need.items():
            if eng == "pe" and s == "s_pe":
                continue
            if kn.get(s, 0) >= v:
                continue
            kn[s] = v
            out.append((s, v))
        return out

    def _mark(self, tok, reads, writes):
        s, v = tok
        for t in reads:
            if t.r.get(s, 0) < v:
                t.r[s] = v
        for t in writes:
            t.w = tok
            t.r = {}

    @staticmethod
    def _expand(writes):
        out = list(writes)
        for t in writes:
            out.extend(t.also)
        return out

    def op(self, eng, fn, reads=(), writes=(), signal=True):
        if self.dry:
            return
        writes = self._expand(writes) + [t for t in reads if t.excl]
        reads = [t for t in reads if not t.excl]
        waits = self._waits(eng, reads, writes)
        if signal:
            self.cnt[eng] += 1
            tok = ("s_" + eng, self.cnt[eng])
        else:
            tok = ("s_" + eng, self.cnt[eng] + 1)
        self._mark(tok, reads, writes)
        lst = self.ops[eng]
        for w in waits:
            lst.append(("wait", w[0], w[1]))
        lst.append(("op", fn, ("s_" + eng) if signal else None, self.tag))

    def dma(self, q, fn, lane, reads=(), writes=()):
        if self.dry:
            return
        writes = self._expand(writes)
        waits = self._waits(q, reads, writes)
        n = self.lanes.get(lane, 0)
        sem = "l_" + lane
        if n > 0 and self.known[q].get(sem, 0) < 16 * n:
            self.known[q][sem] = 16 * n
            waits.append((sem, 16 * n))
        self.lanes[lane] = n + 1
        tok = (sem, 16 * (n + 1))
        self._mark(tok, reads, writes)
        lst = self.ops[q]
        for w in waits:
            lst.append(("wait", w[0], w[1]))
        lst.append(("dma", fn, sem))

    def inherit(self, new_tiles, old_tiles):
        if self.dry:
            return
        acc = {}
        for t in old_tiles:
            if t.w is not None:
                s, v = t.w
                acc[s] = max(acc.get(s, 0), v)
            for s, v in t.r.items():
                acc[s] = max(acc.get(s, 0), v)
        for t in new_tiles:
            t.w = None
            t.r = dict(acc)


def build_program(cfg):
    C = cfg
    D, KC, E, H, NP, NS, NT, HW, NB, NCH, WB = C.D, C.KC, C.E, C.H, C.NP, C.NS, C.NT, C.HW, C.NB, C.NCH, C.WB
    NPASS = C.NPASS
    NTOK = NPASS * NP
    NSAMP = NPASS * NS
    nc = bass.Bass("TRN2", target_bir_lowering=False)
    P = Prog()

    def din(name, shape):
        return nc.dram_tensor(name, list(shape), F32, kind="ExternalInput").ap()

    def dout(name, shape):
        return nc.dram_tensor(name, list(shape), F32, kind="ExternalOutput").ap()

    xp_d = din("xp", [NTOK, D])
    xpre_d = din("xpre", [NTOK, D])
    xs_d = din("xs", [NSAMP, D])
    st_d = din("st", [NSAMP, H, 128, 128])
    w_in_d = din("w_in", [D, C.NCOLS])
    w_pa_d = din("w_pa", [E, D])
    w_pb_d = din("w_pb", [E, D])
    w_o_d = din("w_o", [D, D])
    lbl_d = din("lb_logits", [2, E])
    gpre_d = din("g_pre", [D])
    lng_d = din("ln_g", [E])
    lnb_d = din("ln_b", [E])
    ws_d = din("w_s", [H, 128, 128])
    bs_d = din("b_s", [H, 128])
    gon_d = din("g_onorm", [E])
    gpost_d = din("g_post", [D])
    c_ident_d = din("c_ident", [128, 128])
    c_tri_d = din("c_tri", [128, 128])
    c_mask01_d = din("c_mask01", [128, NP])
    c_maskT_d = din("c_maskT", [128, NP])
    c_sel_d = din("c_sel", [NS, NS * 128])

    yp_d = dout("yp", [NTOK, D])
    ys_d = dout("ys", [NSAMP, D])
    sp_d = dout("sp", [H, 128, 128])
    ss_d = dout("ss", [NSAMP, H, 128, 128])
    vp_d = dout("vp", [128, E])
    vs_d = dout("vs", [NSAMP, E])

    R1_B = KC * NT * 2 + 2 * H * NT * 2
    Y_B = NB * D * 4
    R1_B = max(R1_B, Y_B)
    R3_B = max(KC * NT * 2, 2 * D * 4, 2 * E * 4 + H * 128 * 4 + E * 4)
    WS_B = KC * WB * 2
    NWS = 3
    BSET_B = 8 * NT * 4 + 4 * NP * 2 + 2 * NB * 128 * 2 + 128 * 4 + NS * 128 * 4 + NCH * 128 * 2 + 128 * 4 + NT * 2 + NS * 4
    R3_B = max(R3_B, BSET_B)
    R2_B = max(
        BSET_B,
        D * 4 + D * 2,
        8 * NT * 4 + 4 * NP * 2 + NB * 128 * 2 + NB * 256 * 2 + 256 * 4 + 256 * 2 + NS * 128 * 4 + 64,
        NB * E * 2 + E * 2 + 6 * NT * 4,
        D * 4 + D * 4 + D + 64,
    )
    R2_B = (R2_B + 63) // 64 * 64

    import contextlib
    es = contextlib.ExitStack()

    def sb(name, shape, dt):
        return es.enter_context(nc.sbuf_tensor(name, list(shape), dt))

    R1 = sb("R1", [128, R1_B // 4], F32)
    R2 = sb("R2", [128, R2_B // 4], F32)
    R3 = sb("R3", [128, R3_B // 4], F32)
    WS = [sb("WS%d" % i, [128, KC, WB], BF16) for i in range(NWS)]
    Sf = sb("Sf", [128, H, 128], F32)
    Sb = sb("Sb", [128, H, 128], BF16)
    Sbprev = Sb
    identf = sb("identf", [128, 128], F32)
    identb = sb("identb", [128, 128], BF16)
    trif = sb("trif", [128, 128], F32)
    mask01 = sb("mask01", [128, NP], BF16)
    maskT = sb("maskT", [128, NP], BF16)
    mstage = None
    onesf = sb("onesf", [128, 128], F32)
    sel = sb("sel", [NS, NS * 128], F32)
    gpreT = sb("gpreT", [128, KC], F32)
    lbT = sb("lbT", [128, H], F32)
    omlT = sb("omlT", [128, H], F32)
    lb1T = sb("lb1T", [128, H], F32)
    gonT = sb("gonT", [128, H], F32)
    wmT = sb("wmT", [128, H, 128], BF16)
    Rsm = sb("Rsm", [NS, H, NS], BF16)
    w00 = sb("w00", [NS, H], F32)
    bs0 = sb("bs0", [128, H], F32)
    small = sb("small", [128, 128], F32)
    ssq = sb("ssq", [128, (NB + 1) * (D // 256)], F32)
    junkO = sb("junkO", [128, 256], BF16)
    epsb = sb("epsb", [128, 1], F32)
    mhalf = sb("mhalf", [128, 1], F32)

    ps = es.enter_context(nc.psum_tensor("ps", [128, 8, 512], F32))

    def view(R, off, shape, dt, parts=128):
        esz = 2 if dt == BF16 else 4
        n = int(np.prod(shape[1:]))
        nbytes = n * esz
        assert off % 4 == 0 and nbytes % 4 == 0
        ap = R[0:parts, off // 4:(off + nbytes) // 4]
        if dt != F32:
            ap = ap.bitcast(dt)
        if len(shape) == 3:
            ap = ap.rearrange("p (a b) -> p a b", a=shape[1])
        return ap

    xT = view(R1, 0, [128, KC, NT], BF16)
    ya = view(R1, KC * NT * 2, [128, H, NT], BF16)
    yb = view(R1, KC * NT * 2 + H * NT * 2, [128, H, NT], BF16)
    g_tm = view(R1, KC * NT * 2, [128, NB, E], F32) if NB * E * 4 <= 2 * H * NT * 2 else None
    assert g_tm is not None
    y_tm = view(R1, 0, [128, NB, D], F32)
    wsld2 = view(R1, KC * NT * 2, [128, 128], F32)
    wsall = view(R1, KC * NT * 2, [128, H, 128], F32)
    merged = view(R3, 0, [128, KC, NT], BF16)
    xs2 = view(R3, 0, [128, D], F32)
    xs3 = view(R3, D * 4, [128, D], F32) if 2 * D * 4 <= R3_B else None
    xf = [view(R3, 0, [128, D], F32), view(R3, D * 4, [128, D], F32)]
    lng_b = view(R3, 0, [128, E], F32)
    lnb_b = view(R3, E * 4, [128, E], F32)
    bsb = view(R3, 2 * E * 4, [128, H, 128], F32)
    gs = view(R3, 2 * E * 4 + H * 128 * 4, [NS, E], F32, parts=NS)
    wsld = view(R2, 0, [128, 128], F32)
    mst = view(R2, 512, [128, NP], F32)
    xs1 = view(R2, 0, [128, D], F32)
    junk = view(R2, D * 4, [128, D], BF16)
    class BSet:
        def __init__(self, R, RB, tag):
            o = 0
            self.A = []
            for i in range(8):
                self.A.append(view(R, o, [128, NT], F32)); o += NT * 4
            self.qb = view(R, o, [128, NP], BF16); o += NP * 2
            self.kb = view(R, o, [128, NP], BF16); o += NP * 2
            self.kdec = view(R, o, [128, NP], BF16); o += NP * 2
            self.att_sb = view(R, o, [128, NP], BF16); o += NP * 2
            self.kdec_tm = view(R, o, [128, NB, 128], BF16); o += NB * 128 * 2
            self.i_tm = view(R, o, [128, NB, 128], BF16); o += NB * 128 * 2
            self.i_s = view(R, o, [NS, 128], F32, parts=NS); o += 128 * 4
            self.Sin = view(R, o, [128, NS, 128], F32); o += NS * 128 * 4
            self.Sball = view(R, o, [128, NCH, 128], BF16); o += NCH * 128 * 2
            self.Sg = view(R, o, [128, 128], F32); o += 128 * 4
            self.iTb = view(R, o, [128, NT], BF16); o += NT * 2
            self.iTs = view(R, o, [128, NS], F32); o += NS * 4
            assert o <= RB, (o, RB)
            self.TA = [Tile("A%d%s" % (i, tag)) for i in range(8)]
            self.T_qb, self.T_kb, self.T_kdec, self.T_att = Tile("qb"), Tile("kb"), Tile("kdec"), Tile("att")
            self.T_kdtm = [Tile("kdtm") for _ in range(NB)]
            self.T_itm = [Tile("itm") for _ in range(NB)]
            self.T_is, self.T_Sin = Tile("is"), Tile("Sin")
            self.T_Sball = [Tile("sball") for _ in range(NCH)]
            self.T_Sg = Tile("Sg")
            self.T_iTb, self.T_iTs = Tile("iTb"), Tile("iTs")
            self.tiles = (self.TA + [self.T_qb, self.T_kb, self.T_kdec, self.T_att, self.T_is, self.T_Sin]
                          + self.T_kdtm + self.T_itm + self.T_Sball + [self.T_Sg, self.T_iTb, self.T_iTs])

    o = 0
    vn_bf = view(R2, o, [128, NB, E], BF16); o += NB * E * 2
    vns_bf = view(R2, o, [NS, E], BF16, parts=NS); o += E * 2
    AA = []
    for i in range(6):
        AA.append(view(R2, o, [128, NT], F32)); o += NT * 4
    assert o <= R2_B
    MM = [view(R2, i * NT * 4, [128, NT], F32) for i in range(4)]
    gpost_b = view(R2, 0, [128, D], F32)
    ys_tm = view(R2, D * 4, [NS, D], F32, parts=NS)
    xq = [view(R2, 2 * D * 4, [128, D // 8], F32), view(R2, 2 * D * 4 + D // 2, [128, D // 8], F32)]

    T_xT = [[Tile("xT") for _ in range(NB + 1)] for _ in range(KC)]
    T_ya = [Tile("ya") for _ in range(H)]
    T_yb = [Tile("yb") for _ in range(H)]
    T_g = [Tile("g", also=T_ya + T_yb) for _ in range(NB)]
    for t in T_ya + T_yb:
        t.also = list(T_g)
    T_y = [Tile("y") for _ in range(NB)]
    T_mg = [Tile("merged") for _ in range(KC)]
    T_WS = [(Tile("wsa"), Tile("wsb")) for _ in range(NWS)]
    T_ps = [Tile("psb%d" % b, excl=True) for b in range(8)]
    T_q7 = [T_ps[4], T_ps[4], T_ps[4], T_ps[4]]
    T_Sf = [Tile("Sf") for _ in range(H)]
    T_Sb = [Tile("Sb") for _ in range(H)]
    T_Sbprev = T_Sb
    T_const = Tile("const")
    T_small = Tile("small")
    T_smx = [Tile("smx0"), Tile("smx1")]
    BS = [BSet(R2, R2_B, "a"), BSet(R3, R3_B, "b")]
    TAA = [Tile("AA%d" % i) for i in range(6)]
    T_vn = [Tile("vn") for _ in range(NB)]
    T_vns = Tile("vns")
    A_tiles = TAA + T_vn + [T_vns]
    T_lnc, T_bsb, T_gs = Tile("lnc"), Tile("bsb"), Tile("gs")
    A3_tiles = [T_lnc, T_bsb, T_gs]
    TM = [Tile("M%d" % i) for i in range(4)]
    T_gpost, T_ys, T_junkO = Tile("gpost"), Tile("ys"), Tile("junkO")
    T_xq = [Tile("xq0"), Tile("xq1")]
    T_xf = [Tile("xf0"), Tile("xf1")]
    T_ssq = [Tile("ssq") for _ in range(NB + 1)]
    T_yq = [[Tile("yq") for _ in range(4)] for _ in range(NB + 1)]
    O_tiles = [T_gpost, T_ys] + T_xq
    T_xs = [Tile("xs1"), Tile("xs2"), Tile("xs3")]
    T_junk = Tile("junk")
    T_wsld = Tile("wsld")
    T_wsld2 = Tile("wsld2")
    for _t in T_g:
        _t.also.append(T_wsld2)
    T_wm = [Tile("wm") for _ in range(H)]
    region = {"r2": [T_wsld], "r3": []}

    def switch(reg, new_tiles):
        P.inherit(new_tiles, region[reg])
        region[reg] = list(new_tiles)

    sp_e, act_e, dve_e, pe_e, pool_e = nc.sync, nc.scalar, nc.vector, nc.tensor, nc.gpsimd

    def ACT(out, in_, func, reads, writes, **kw):
        P.op("act", lambda: act_e.activation(out=out, in_=in_, func=func, **kw), reads, writes)

    def TS(out, in0, s1, s2, op0, op1, reads, writes, eng="dve"):
        e = dve_e if eng == "dve" else pool_e
        if op1 is None:
            P.op(eng, lambda: e.tensor_scalar(out=out, in0=in0, scalar1=s1, scalar2=None, op0=op0), reads, writes)
        else:
            P.op(eng, lambda: e.tensor_scalar(out=out, in0=in0, scalar1=s1, scalar2=s2, op0=op0, op1=op1), reads, writes)

    def TT(out, in0, in1, op, reads, writes, eng="dve"):
        e = dve_e if eng == "dve" else pool_e
        P.op(eng, lambda: e.tensor_tensor(out=out, in0=in0, in1=in1, op=op), reads, writes)

    def STT(out, in0, scalar, in1, op0, op1, reads, writes):
        P.op("dve", lambda: dve_e.scalar_tensor_tensor(out=out, in0=in0, scalar=scalar, in1=in1, op0=op0, op1=op1), reads, writes)

    def CP(out, in_, reads, writes, eng="dve"):
        e = dve_e if eng == "dve" else pool_e
        P.op(eng, lambda: e.tensor_copy(out=out, in_=in_), reads, writes)

    def MM_(out, lhsT, rhs, start, stop, reads, writes, signal):
        old = P.tag
        if lhsT.dtype == F32:
            P.tag = old + "#fp32"
        P.op("pe", lambda: pe_e.matmul(out, lhsT, rhs, start=start, stop=stop), reads, writes, signal=signal)
        P.tag = old

    def TR(out, in_, ident, reads, writes, signal=True):
        P.op("pe", lambda: pe_e.transpose(out, in_, ident), reads, writes, signal=signal)

    def DMA(q, out, in_, lane, reads, writes, **kw):
        e = {"sp": sp_e, "pool": pool_e, "act": act_e}[q]
        P.dma(q, lambda: e.dma_start(out=out, in_=in_, **kw), lane, reads, writes)

    class WStream:
        def __init__(self):
            self.specs = []
            self.issued = 0
            self.n = 0
            self.done_set = set()

        def reset(self):
            self.n = 0
            self.issued = 0
            self.done_set = set()

        def get(self, segs, kc):
            i = self.n
            self.n += 1
            if P.dry:
                self.specs.append((segs, kc))
                return WS[i % NWS], T_WS[i % NWS], i
            self._pump()
            assert self.issued > i, "weight block %d requested before its slot was released" % i
            return WS[i % NWS], T_WS[i % NWS], i

        def done(self, i):
            if P.dry:
                return
            self.done_set.add(i)
            self._pump()

        def _pump(self):
            while self.issued < len(self.specs) and (self.issued < NWS or (self.issued - NWS) in self.done_set):
                self._issue(self.issued)
                self.issued += 1

        def _issue(self, j):
            segs, kc = self.specs[j]
            slot = j % NWS
            c0 = 0
            for sg in segs:
                ncols = sg.shape[1]
                src = sg.rearrange("(kc p) n -> p kc n", p=128)
                if len(segs) == 1:
                    DMA("pool", WS[slot][:, 0:kc, c0:c0 + ncols], src, "ws%da" % slot, [], list(T_WS[slot]))
                else:
                    assert len(segs) == 2 and ncols == 128
                    k = c0 // 128
                    DMA("pool", WS[slot][:, 0:kc, c0:c0 + ncols], src, "ws%d%s" % (slot, "ab"[k]), [], [T_WS[slot][k]])
                c0 += ncols

    wst = WStream()

    rot = {"fm": 0, "tm": 0, "o": 0, "q7": 0, "nfm": 2, "ntm": 4}

    def ps_fm():
        k = rot["fm"] % rot["nfm"]; rot["fm"] = k + 1
        return 2 * k

    def ps_tm():
        k = rot["tm"] % rot["ntm"]; rot["tm"] = k + 1
        return k

    def ps_q7():
        k = rot["q7"]; rot["q7"] = (k + 1) % 3
        return k

    def fm_view(sbuf2d):
        return sbuf2d.rearrange("p (a b) -> p a b", a=2)

    def fm_matmul(Wap, Wt, c0, src, src_tiles, kcn):
        b = ps_fm()
        for kc in range(kcn):
            for hf in range(2):
                MM_(ps[:, b + hf, 0:HW], Wap[:, kc, c0:c0 + 128], src[:, kc, hf * HW:(hf + 1) * HW],
                    kc == 0, kc == kcn - 1, [Wt[0] if c0 < 128 else Wt[1]] + src_tiles(kc), [T_ps[b + hf]],
                    signal=(kc == kcn - 1 and hf == 1))
        return b

    def xT_tiles(kc):
        return T_xT[kc]

    def setup():
        P.tag = "setup"
        DMA("act", identf[:], c_ident_d, "cs1", [], [T_const])
        DMA("act", trif[:], c_tri_d, "cs2", [], [T_const])
        DMA("act", mst[:], c_mask01_d, "c0", [], [T_wsld])
        CP(mask01[:], mst[:], [T_wsld], [T_const])
        DMA("act", mst[:], c_maskT_d, "c0", [], [T_wsld])
        CP(maskT[:], mst[:], [T_wsld], [T_const])
        DMA("act", sel[:], c_sel_d, "cs3", [], [T_const])
        DMA("act", gpreT[:], gpre_d.rearrange("(k p) -> p k", p=128), "cs0", [], [T_const])
        DMA("act", lbT[:], lbl_d[0, :].rearrange("(h p) -> p h", p=128), "cs1", [], [T_const])
        DMA("act", lb1T[:], lbl_d[1, :].rearrange("(h p) -> p h", p=128), "cs2", [], [T_const])
        DMA("act", gonT[:], gon_d.rearrange("(h p) -> p h", p=128), "cs3", [], [T_const])
        DMA("act", bs0[:], bs_d[:, 0].partition_broadcast(128), "cs0", [], [T_const])
        DMA("act", w00[:], ws_d[:, 0, 0].partition_broadcast(NS), "cs1", [], [T_const])
        P.op("dve", lambda: dve_e.memset(onesf[:], 1.0), [], [T_const])
        P.op("dve", lambda: dve_e.memset(epsb[:], EPS), [], [T_const])
        P.op("dve", lambda: dve_e.memset(mhalf[:], -0.5), [], [T_const])
        P.op("dve", lambda: dve_e.memset(small[:], 0.0), [], [T_small])
        CP(identb[:], identf[:], [T_const], [T_const])
        TT(lbT[:], lbT[:], lb1T[:], ALU.subtract, [T_const], [T_const])
        ACT(lbT[:], lbT[:], AF.Sigmoid, [T_const], [T_const])
        TS(omlT[:], lbT[:], -1.0, 1.0, ALU.mult, ALU.add, [T_const], [T_const])
        for g in range(H):
            TS(Rsm[:, g, :], identf[0:NS, 0:NS], w00[:, g:g + 1], None, ALU.mult, None, [T_const], [T_const])
        DMA("pool", wsall[:, :, :], ws_d.rearrange("g t s -> t g s"), "c1", [], [T_wsld2])
        G4 = min(4, H)
        for g4 in range(H // G4):
            bk = ps_tm()
            for k in range(G4):
                TR(ps[:, bk, k * 128:(k + 1) * 128], wsall[:, g4 * G4 + k, :], identf[:], [T_wsld2, T_const], [T_ps[bk]],
                   signal=(k == G4 - 1))
            TT(wmT[:, g4 * G4:(g4 + 1) * G4, :], ps[:, bk, 0:G4 * 128].rearrange("p (a b) -> p a b", a=G4),
               trif[:].unsqueeze(1).broadcast_to([128, G4, 128]), ALU.mult, [T_ps[bk], T_const],
               [T_wm[g] for g in range(g4 * G4, (g4 + 1) * G4)])
        P.op("dve", lambda: dve_e.memset(xT[:, :, NP:NT], 0.0), [], [T_xT[kc][NB] for kc in range(KC)])
        P.op("dve", lambda: dve_e.memset(Sf[:], 0.0), [], T_Sf)
        P.op("dve", lambda: dve_e.memset(Sb[:], 0.0), [], T_Sb)

    def phase_X(x_dram, row0, srow0, with_samples, vhook=None):
        P.tag = "phase_X"
        bufs = [xs1, xs2] + ([xs3] if xs3 is not None else [])
        nbuf = len(bufs)
        tiles = [(x_dram[row0 + j * 128: row0 + (j + 1) * 128, :], 128, j) for j in range(NB)]
        if with_samples:
            tiles.append((xs_d[srow0: srow0 + NS, :], NS, NB))
        for n, (src, rows, j) in enumerate(tiles):
            xb, Tx = bufs[n % nbuf], T_xs[n % nbuf]
            DMA("sp", xb[0:rows, :], src, "x%d" % (n % nbuf), [], [Tx])
            sa = small[0:rows, 80 + 2 * (n % 2): 81 + 2 * (n % 2)]
            sc = small[0:rows, 81 + 2 * (n % 2): 82 + 2 * (n % 2)]
            Tsm = T_smx[n % 2]
            ACT(junk[0:rows, :], xb[0:rows, :], AF.Square, [Tx], [T_junk, Tsm], accum_out=sa)
            ACT(sc, sa, AF.Copy, [Tsm], [Tsm])
            TS(sc, sc, 1.0 / D, EPS, ALU.mult, ALU.add, [Tsm], [Tsm], eng="pool")
            P.op("pool", lambda sc=sc, rows=rows: pool_e.tensor_tensor(out=sc, in0=sc, in1=mhalf[0:rows, 0:1], op=ALU.pow),
                 [Tsm, T_const], [Tsm])
            TS(xb[0:rows, :], xb[0:rows, :], sc, 0.0, ALU.mult, ALU.add, [Tx, Tsm], [Tx], eng="pool")
            col0 = j * 128 if rows == 128 else NP
            for k4 in range(KC // 4):
                b = ps_tm()
                for kk in range(4):
                    kc = k4 * 4 + kk
                    TR(ps[:, b, kk * 128: kk * 128 + rows], xb[0:rows, kc * 128:(kc + 1) * 128],
                       identf[0:rows, 0:rows], [Tx, T_const], [T_ps[b]], signal=(kk == 3))
                src_ps = ps[:, b, :].rearrange("p (a b) -> p a b", a=4)[:, :, 0:rows]
                gp = gpreT[:, k4 * 4:(k4 + 1) * 4].unsqueeze(2).broadcast_to([128, 4, rows])
                TT(xT[:, k4 * 4:(k4 + 1) * 4, col0:col0 + rows], src_ps, gp, ALU.mult,
                   [T_ps[b], T_const], [T_xT[kc][j] for kc in range(k4 * 4, k4 * 4 + 4)])
            if vhook is not None and rows == 128:
                vhook(j)
                P.tag = "phase_X"

    BK_D0, BK_D1, BK_ATT, BK_O = 4, 5, 6, 7

    def proj_i(S_, Wap, Wt, with_samples, ic0):
        bI = fm_matmul(Wap, Wt, ic0, xT, xT_tiles, KC)
        TpI = [T_ps[bI], T_ps[bI + 1]]
        ACT(fm_view(S_.iTb), ps[:, bI:bI + 2, 0:HW], AF.Copy, TpI, [S_.T_iTb])
        if with_samples:
            ACT(S_.iTs[:, :], ps[:, bI + 1, NP - HW:NT - HW], AF.Copy, TpI, [S_.T_iTs])

    def proj_i_tr(S_, with_samples):
        P.tag = "stage_P1"
        b = BK_O
        for j in range(NB):
            dst = ps[:, b, j * 64:(j + 1) * 64].bitcast(BF16)[:, 0:128]
            TR(dst, S_.iTb[:, j * 128:(j + 1) * 128], identb[:], [S_.T_iTb, T_const], [T_ps[b]],
               signal=(j == NB - 1 and not with_samples))
        if with_samples:
            TR(ps[0:NS, b, 256:384], S_.iTs[:, :], identf[:], [S_.T_iTs, T_const], [T_ps[b]])
        dsta = ps[:, b, 0:NB * 64].bitcast(BF16).rearrange("p (j d) -> p j d", d=128)
        ACT(S_.i_tm[:, :, :], dsta, AF.Copy, [T_ps[b]], S_.T_itm)
        if with_samples:
            ACT(S_.i_s[:, :], ps[0:NS, b, 256:384], AF.Copy, [T_ps[b]], [S_.T_is])

    def gate_math_a(S_, h, bF):
        A, TA = S_.A, S_.TA
        Tp = [T_ps[bF], T_ps[bF + 1]]
        ACT(fm_view(A[1]), ps[:, bF:bF + 2, 0:HW], AF.Sigmoid, Tp, [TA[1]])
        ACT(A[2], A[1], AF.Ln, [TA[1], T_const], [TA[2]], scale=omlT[:, h:h + 1], bias=lbT[:, h:h + 1])
        P.op("dve", lambda: dve_e.tensor_tensor_scan(out=A[3][:, 0:NP], data0=mask01[:], data1=A[2][:, 0:NP],
                                                      initial=0.0, op0=ALU.mult, op1=ALU.add),
             [TA[2], T_const], [TA[3]])
        TS(A[1], A[1], omlT[:, h:h + 1], lbT[:, h:h + 1], ALU.mult, ALU.add, [TA[1], T_const], [TA[1]])
        TS(A[7], A[1], -1.0, 1.0, ALU.mult, ALU.add, [TA[1]], [TA[7]])

    def gate_math_b(S_, h, full):
        A, TA = S_.A, S_.TA
        ACT(A[5][:, 0:NP], A[3][:, 0:NP], AF.Exp, [TA[3]], [TA[5]])
        ACT(A[6][:, 0:NP], A[3][:, 0:NP], AF.Exp, [TA[3]], [TA[6]], scale=-1.0)
        TT(A[6][:, 0:NP], A[7][:, 0:NP], A[6][:, 0:NP], ALU.mult, [TA[7], TA[6]], [TA[6]])
        ebl = A[5][:, 0:NP].rearrange("p (c s) -> p c s", s=64)[:, :, 63:64].broadcast_to([128, NCH, 64])
        TT(S_.kdec.rearrange("p (c s) -> p c s", s=64), A[6][:, 0:NP].rearrange("p (c s) -> p c s", s=64), ebl,
           ALU.mult, [TA[6], TA[5]], [S_.T_kdec])
        if full:
            CP(S_.kb[:], A[6][:, 0:NP], [TA[6]], [S_.T_kb])

    def stage_P1a(S_, h, srow0, prefix):
        P.tag = "stage_P1"
        if prefix:
            W, Wt, wi = wst.get([w_in_d[:, C.cF + h * 128: C.cF + (h + 1) * 128],
                                 w_in_d[:, C.cI + h * 128: C.cI + (h + 1) * 128]], KC)
            S_.wblocks = [(W, Wt, wi)]
            proj_i(S_, W, Wt, False, 128)
            return
        W1, W1t, w1 = wst.get([w_in_d[:, C.cI + h * 128: C.cI + (h + 1) * 128],
                               w_in_d[:, C.cQ + h * 128: C.cQ + (h + 1) * 128]], KC)
        S_.wblocks = [(W1, W1t, w1)]
        DMA("sp", S_.Sin[:], st_d[srow0:srow0 + NS, h].rearrange("s d e -> d s e"), "sin", [], [S_.T_Sin])
        proj_i(S_, W1, W1t, True, 0)

    def stage_P1b(S_, h, prefix):
        P.tag = "stage_P1"
        if prefix:
            return
        (W1, W1t, w1) = S_.wblocks[0]
        bQ = fm_matmul(W1, W1t, 128, xT, xT_tiles, KC)
        wst.done(w1)
        ACT(fm_view(S_.A[0]), ps[:, bQ:bQ + 2, 0:HW], AF.Silu, [T_ps[bQ], T_ps[bQ + 1]], [S_.TA[0]])

    def stage_P2(S_, h, prefix):
        P.tag = "stage_P2"
        if prefix:
            W, Wt, wi = S_.wblocks[0]
            bF = fm_matmul(W, Wt, 0, xT, xT_tiles, KC)
            wst.done(wi)
            gate_math_a(S_, h, bF)
            gate_math_b(S_, h, False)
            return
        W2, W2t, w2 = wst.get([w_in_d[:, C.cF + h * 128: C.cF + (h + 1) * 128],
                               w_in_d[:, C.cZB + h * 128: C.cZB + (h + 1) * 128]], KC)
        bF = fm_matmul(W2, W2t, 0, xT, xT_tiles, KC)
        gate_math_a(S_, h, bF)
        bZ = fm_matmul(W2, W2t, 128, xT, xT_tiles, KC)
        wst.done(w2)
        ACT(fm_view(S_.A[4]), ps[:, bZ:bZ + 2, 0:HW], AF.Silu, [T_ps[bZ], T_ps[bZ + 1]], [S_.TA[4]])
        gate_math_b(S_, h, True)
        TT(S_.qb[:], S_.A[0][:, 0:NP], S_.A[5][:, 0:NP], ALU.mult, [S_.TA[0], S_.TA[5]], [S_.T_qb])

    def stage_TR(S_, h):
        P.tag = "stage_B2"
        for j in range(NB):
            dst = ps[:, BK_ATT, j * 64:(j + 1) * 64].bitcast(BF16)[:, 0:128]
            TR(dst, S_.kdec[:, j * 128:(j + 1) * 128], identb[:], [S_.T_kdec, T_const], [T_ps[BK_ATT]], signal=(j == NB - 1))
        dsta = ps[:, BK_ATT, 0:NB * 64].bitcast(BF16).rearrange("p (j d) -> p j d", d=128)
        ACT(S_.kdec_tm[:, :, :], dsta, AF.Copy, [T_ps[BK_ATT]], S_.T_kdtm)

    def stage_B2(S_, h, prefix):
        P.tag = "stage_B2"
        A, TA = S_.A, S_.TA
        if not prefix:
            for j in range(NB):
                MM_(ps[:, BK_ATT, j * 128:(j + 1) * 128], S_.kb[:, j * 128:(j + 1) * 128], S_.qb[:, j * 128:(j + 1) * 128],
                    True, True, [S_.T_kb, S_.T_qb], [T_ps[BK_ATT]], signal=(j == NB - 1))
            TT(S_.att_sb[:], ps[:, BK_ATT, 0:NP], maskT[:], ALU.mult, [T_ps[BK_ATT], T_const], [S_.T_att])
        assert NCH <= 8
        for c in range(NCH):
            j, half = c // 2, c % 2
            pr = slice(64 * half, 64 * half + 64)
            bank = BK_D0 if half == 0 else BK_D1
            MM_(ps[:, bank, j * 128:(j + 1) * 128], S_.kdec_tm[pr, j, :], S_.i_tm[pr, j, :], True, True,
                [S_.T_kdtm[j], S_.T_itm[j]], [T_ps[bank]], signal=(c == NCH - 1))
        assert NCH % 2 == 0
        for c in range(NCH):
            j, half = c // 2, c % 2
            bank = BK_D0 if half == 0 else BK_D1
            if c % 2 == 0:
                src, Tsrc, dst, Tdst = Sf[:, h, :], T_Sf[h], S_.Sg[:, :], S_.T_Sg
            else:
                src, Tsrc, dst, Tdst = S_.Sg[:, :], S_.T_Sg, Sf[:, h, :], T_Sf[h]
            STT(dst, src, A[5][:, c * 64 + 63: c * 64 + 64], ps[:, bank, j * 128:(j + 1) * 128],
                ALU.mult, ALU.add, [Tsrc, TA[5], T_ps[bank]], [Tdst])
            if not prefix:
                ACT(S_.Sball[:, c, :], dst, AF.Copy, [Tdst], [S_.T_Sball[c]])
            elif c == NCH - 1:
                ACT(Sb[:, h, :], dst, AF.Copy, [Tdst], [T_Sb[h]])

    def stage_B2b(S_, h):
        P.tag = "stage_B2b"
        A, TA = S_.A, S_.TA
        for s in range(NS):
            ACT(S_.Sin[:, s, :], S_.Sin[:, s, :], AF.Identity, [S_.T_Sin, TA[1]], [S_.T_Sin], scale=A[1][:, NP + s:NP + s + 1])
        for g in range((NS + 3) // 4):
            bank = BK_D0 if g % 2 == 0 else BK_D1
            ss_ = list(range(g * 4, min(NS, g * 4 + 4)))
            for s in ss_:
                MM_(ps[:, bank, (s % 4) * 128:(s % 4 + 1) * 128], sel[:, s * 128:(s + 1) * 128], S_.i_s[:, :], True, True,
                    [T_const, S_.T_is], [T_ps[bank]], signal=(s == ss_[-1]))
            for s in ss_:
                STT(S_.Sin[:, s, :], ps[:, bank, (s % 4) * 128:(s % 4 + 1) * 128], A[7][:, NP + s:NP + s + 1], S_.Sin[:, s, :],
                    ALU.mult, ALU.add, [T_ps[bank], TA[7], S_.T_Sin], [S_.T_Sin])

    def stage_B3(S_, h, srow0):
        P.tag = "stage_B3"
        A, TA = S_.A, S_.TA
        for c in range(NCH):
            j = c // 2
            if c % 2 == 0:
                MM_(ps[:, BK_O, j * 128:(j + 1) * 128], S_.i_tm[:, j, :], S_.att_sb[:, j * 128:(j + 1) * 128], True, False,
                    [S_.T_itm[j], S_.T_att], [T_ps[BK_O]], False)
            if c == 0:
                sbc, Tsbc = Sbprev[:, h, :], T_Sbprev[h]
            else:
                sbc, Tsbc = S_.Sball[:, c - 1, :], S_.T_Sball[c - 1]
            MM_(ps[:, BK_O, c * 64:(c + 1) * 64], sbc, S_.qb[:, c * 64:(c + 1) * 64], False, c % 2 == 1,
                [Tsbc, S_.T_qb], [T_ps[BK_O]], signal=(c % 2 == 1))
        ACT(Sb[:, h, :], S_.Sball[:, NCH - 1, :], AF.Copy, [S_.T_Sball[NCH - 1]], [T_Sb[h]])
        for s in range(NS):
            MM_(ps[:, BK_ATT, s:s + 1], S_.Sin[:, s, :], A[0][:, NP + s:NP + s + 1], True, True,
                [S_.T_Sin, TA[0]], [T_ps[BK_ATT]], signal=(s == NS - 1))
        DMA("sp", ss_d[srow0:srow0 + NS, h].rearrange("s d e -> d s e"), S_.Sin[:], "sout", [S_.T_Sin], [])
        ACT(A[0][:, 0:NP], ps[:, BK_O, 0:NP], AF.Copy, [T_ps[BK_O], TA[0]], [TA[0]])
        ACT(A[0][:, NP:NT], ps[:, BK_ATT, 0:NS], AF.Copy, [T_ps[BK_ATT], TA[0]], [TA[0]])
        ACT(A[2], A[0], AF.Square, [TA[0]], [TA[2]])

    def stage_N(S_, h):
        P.tag = "stage_N"
        A, TA = S_.A, S_.TA
        bb = ps_fm()
        for hf in range(2):
            MM_(ps[:, bb + hf, 0:HW], onesf[:], A[2][:, hf * HW:(hf + 1) * HW], True, True,
                [T_const, TA[2]], [T_ps[bb + hf]], True)
        ACT(fm_view(A[3]), ps[:, bb:bb + 2, 0:HW], AF.Identity, [T_ps[bb], T_ps[bb + 1]], [TA[3]],
            scale=1.0 / 128, bias=epsb[:, 0:1])
        ACT(A[3], A[3], AF.Ln, [TA[3]], [TA[3]])
        ACT(A[3], A[3], AF.Exp, [TA[3]], [TA[3]], scale=-0.5)
        TT(A[6], A[0], A[3], ALU.mult, [TA[0], TA[3]], [TA[6]])
        STT(yb[:, h, :], A[6], gonT[:, h:h + 1], A[4], ALU.mult, ALU.mult, [TA[6], TA[4], T_const], [T_yb[h]])


    def prep_wm_load(g):
        P.tag = "setup"
        DMA("sp", wsld2[:], ws_d[g], "c1", [], [T_wsld2])

    def prep_wm(g):
        P.tag = "setup"
        TR(ps[:, BK_O, 0:128], wsld2[:], identf[:], [T_wsld2, T_const], [T_ps[BK_O]])
        TT(wmT[:, g, :], ps[:, BK_O, 0:128], trif[:], ALU.mult, [T_ps[BK_O], T_const], [T_wm[g]])

    def phase_prefix(first=False):
        stage_P1a(BS[0], 0, 0, True)
        stage_P2(BS[0], 0, True)
        proj_i_tr(BS[0], False)
        for h in range(H):
            S_ = BS[h % 2]
            Sn = BS[(h + 1) % 2]
            if h + 1 < H:
                stage_P1a(Sn, h + 1, 0, True)
            stage_TR(S_, h)
            if h + 1 < H:
                stage_P2(Sn, h + 1, True)
                proj_i_tr(Sn, False)
            stage_B2(S_, h, True)

    def phase_B(srow0):
        stage_P1a(BS[0], 0, srow0, False)
        stage_P1b(BS[0], 0, False)
        proj_i_tr(BS[0], True)
        stage_P2(BS[0], 0, False)
        for h in range(H):
            S_ = BS[h % 2]
            Sn = BS[(h + 1) % 2]
            stage_B2b(S_, h)
            if h + 1 < H:
                stage_P1a(Sn, h + 1, srow0, False)
            if h > 0:
                stage_N(Sn, h - 1)
            stage_TR(S_, h)
            if h + 1 < H:
                stage_P1b(Sn, h + 1, False)
                proj_i_tr(Sn, True)
            stage_B2(S_, h, False)
            if h + 1 < H:
                stage_P2(Sn, h + 1, False)
            stage_B3(S_, h, srow0)
        stage_N(BS[(H - 1) % 2], H - 1)

    def gelu_from(out, src, rd, wr, t1, Tt1, t2, Tt2):
        ACT(out, src, AF.Gelu_apprx_tanh, rd, wr)

    def vproj_prompt(Wap, Wt, blk, j):
        P.tag = "phase_A"
        b = ps_tm()
        for kc in range(KC):
            MM_(ps[:, b, 0:256], xT[:, kc, j * 128:(j + 1) * 128], Wap[:, kc, 0:256], kc == 0, kc == KC - 1,
                [Wt[0], Wt[1], T_xT[kc][j]], [T_ps[b]], signal=(kc == KC - 1))
        ACT(g_tm[:, j, blk * 256:(blk + 1) * 256], ps[:, b, 0:256], AF.Gelu_apprx_tanh, [T_ps[b]], [T_g[j]])

    def phase_A(srow0, last, vpre=()):
        P.tag = "phase_A"
        DMA("sp", lng_b[:], lng_d.partition_broadcast(128), "c0", [], [T_lnc])
        DMA("sp", lnb_b[:], lnb_d.partition_broadcast(128), "c0", [], [T_lnc])
        DMA("sp", bsb[:], bs_d.rearrange("g t -> (g t)").partition_broadcast(128), "c0", [], [T_bsb])
        def vproj(Wap, Wt, blk, j):
            rows = 128 if j < NB else NS
            col0 = j * 128 if j < NB else NP
            b = ps_tm()
            for kc in range(KC):
                MM_(ps[0:rows, b, 0:256], xT[:, kc, col0:col0 + rows], Wap[:, kc, 0:256], kc == 0, kc == KC - 1,
                    [Wt[0], Wt[1], T_xT[kc][j]], [T_ps[b]], signal=(kc == KC - 1))
            dst = g_tm[:, j, blk * 256:(blk + 1) * 256] if j < NB else gs[:, blk * 256:(blk + 1) * 256]
            Td = T_g[j] if j < NB else T_gs
            ACT(dst, ps[0:rows, b, 0:256], AF.Gelu_apprx_tanh, [T_ps[b]], [Td])

        for blk in range(E // 256):
            if blk < len(vpre):
                Wap, Wt, wi = vpre[blk]
                vproj(Wap, Wt, blk, NB)
                wst.done(wi)
                continue
            Wap, Wt, wi = wst.get([w_in_d[:, C.cV + blk * 256: C.cV + (blk + 1) * 256]], KC)
            for j in range(NB + 1):
                vproj(Wap, Wt, blk, j)
            wst.done(wi)
        nst = (E + 511) // 512
        ln_copies = []
        for j in range(NB + 1):
            rows = 128 if j < NB else NS
            G = g_tm[:, j, :] if j < NB else gs[:, :]
            Td = T_g[j] if j < NB else T_gs
            c0 = 8 + 32 * (j % 2)
            assert 6 * nst <= 24
            Tsm = T_smx[j % 2]
            st6 = small[0:rows, c0:c0 + 6 * nst]
            for k in range(nst):
                w = min(512, E - k * 512)
                P.op("dve", lambda k=k, w=w, G=G, rows=rows, c0=c0: dve_e.bn_stats(
                    out=small[0:rows, c0 + 6 * k: c0 + 6 + 6 * k], in_=G[:, k * 512: k * 512 + w]), [Td], [Tsm])
            mv = small[0:rows, 64 + 4 * (j % 2): 66 + 4 * (j % 2)]
            P.op("dve", lambda mv=mv, st6=st6: dve_e.bn_aggr(out=mv, in_=st6), [Tsm], [Tsm])
            mean = small[0:rows, 64 + 4 * (j % 2): 65 + 4 * (j % 2)]
            rs = small[0:rows, 65 + 4 * (j % 2): 66 + 4 * (j % 2)]
            nmr = small[0:rows, 66 + 4 * (j % 2): 67 + 4 * (j % 2)]
            TS(rs, rs, 1.0, EPS, ALU.mult, ALU.add, [Tsm], [Tsm], eng="pool")
            P.op("pool", lambda rs=rs, rows=rows: pool_e.tensor_tensor(out=rs, in0=rs, in1=mhalf[0:rows, 0:1], op=ALU.pow),
                 [Tsm, T_const], [Tsm])
            P.op("pool", lambda nmr=nmr, mean=mean, rs=rs: pool_e.tensor_tensor(out=nmr, in0=mean, in1=rs, op=ALU.mult),
                 [Tsm], [Tsm])
            TS(nmr, nmr, -1.0, 0.0, ALU.mult, ALU.add, [Tsm], [Tsm], eng="pool")
            TS(G, G, rs, nmr, ALU.mult, ALU.add, [Td, Tsm], [Td])
            TT(G, G, lng_b[0:rows, :], ALU.mult, [Td, T_lnc], [Td], eng="pool")
            TT(G, G, lnb_b[0:rows, :], ALU.add, [Td, T_lnc], [Td])
            if j == NB:
                DMA("sp", vs_d[srow0:srow0 + NS, :], G, "vout", [Td], [])
            elif last and j == NB - 1:
                DMA("sp", vp_d[:, :], G, "vout", [Td], [])
            ln_copies.append((j, G, Td))

        def emit_ln_copies():
            for (j, G, Td) in ln_copies:
                if j == NB:
                    ACT(vns_bf[:, :], G, AF.Copy, [Td], [T_vns])
                else:
                    ACT(vn_bf[:, j, :], G, AF.Copy, [Td], [T_vn[j]])

        for gp in range(H // 2):
            W, Wt, wi = wst.get([w_in_d[:, C.cU + gp * 256: C.cU + (gp + 1) * 256]], KC)
            for gg in range(2):
                bU = fm_matmul(W, Wt, gg * 128, xT, xT_tiles, KC)
                ACT(fm_view(AA[2 + gg]), ps[:, bU:bU + 2, 0:HW], AF.Gelu_apprx_tanh, [T_ps[bU], T_ps[bU + 1]], [TAA[2 + gg]])
            wst.done(wi)
            W, Wt, wi = wst.get([w_in_d[:, C.cZA + gp * 256: C.cZA + (gp + 1) * 256]], KC)
            for gg in range(2):
                bZ = fm_matmul(W, Wt, gg * 128, xT, xT_tiles, KC)
                ACT(fm_view(AA[4 + gg]), ps[:, bZ:bZ + 2, 0:HW], AF.Silu, [T_ps[bZ], T_ps[bZ + 1]], [TAA[4 + gg]])
            wst.done(wi)
            if gp == 0:
                emit_ln_copies()
            for gg in range(2):
                g = gp * 2 + gg
                for j in range(NB):
                    MM_(ps[:, 6, j * 128:(j + 1) * 128], vn_bf[:, j, g * 128:(g + 1) * 128], wmT[:, g, :], True, True,
                        [T_vn[j], T_wm[g]], [T_ps[6]], signal=(j == NB - 1))
                q = ps_q7()
                MM_(ps[:, 4, q * 128: q * 128 + NS], vns_bf[:, g * 128:(g + 1) * 128], Rsm[:, g, :], True, True,
                    [T_vns, T_const], [T_q7[q]], True)
                msb = AA[gg]
                TT(msb[:, 0:NP].rearrange("p (j t) -> p j t", t=128), ps[:, 6, 0:NP].rearrange("p (j t) -> p j t", t=128),
                   bsb[:, g, :].unsqueeze(1).broadcast_to([128, NB, 128]), ALU.add, [T_ps[6], T_bsb], [TAA[gg]])
                TS(msb[:, NP:NT], ps[:, 4, q * 128: q * 128 + NS], bs0[:, g:g + 1], None, ALU.add, None,
                   [T_q7[q], T_const, TAA[gg]], [TAA[gg]])
                TT(AA[2 + gg], AA[2 + gg], AA[gg], ALU.mult, [TAA[2 + gg], TAA[gg]], [TAA[2 + gg]])
                TT(ya[:, g, :], AA[2 + gg], AA[4 + gg], ALU.mult, [TAA[2 + gg], TAA[4 + gg]], [T_ya[g]])

    def phase_M():
        P.tag = "phase_M"
        for cp in range(D // 256):
            W, Wt, wi = wst.get([w_in_d[:, C.cGA + cp * 256: C.cGA + (cp + 1) * 256]], KC)
            for cc in range(2):
                b = fm_matmul(W, Wt, cc * 128, xT, xT_tiles, KC)
                ACT(fm_view(MM[cc]), ps[:, b:b + 2, 0:HW], AF.Sigmoid, [T_ps[b], T_ps[b + 1]], [TM[cc]])
            wst.done(wi)
            W, Wt, wi = wst.get([w_pa_d[:, cp * 256:(cp + 1) * 256]], H)
            for cc in range(2):
                b = fm_matmul(W, Wt, cc * 128, ya, lambda kc: [T_ya[kc]], H)
                TT(fm_view(MM[cc]), ps[:, b:b + 2, 0:HW], fm_view(MM[cc]), ALU.mult, [T_ps[b], T_ps[b + 1], TM[cc]], [TM[cc]])
            wst.done(wi)
            W, Wt, wi = wst.get([w_in_d[:, C.cGB + cp * 256: C.cGB + (cp + 1) * 256]], KC)
            for cc in range(2):
                b = fm_matmul(W, Wt, cc * 128, xT, xT_tiles, KC)
                ACT(fm_view(MM[2 + cc]), ps[:, b:b + 2, 0:HW], AF.Sigmoid, [T_ps[b], T_ps[b + 1]], [TM[2 + cc]])
            wst.done(wi)
            W, Wt, wi = wst.get([w_pb_d[:, cp * 256:(cp + 1) * 256]], H)
            for cc in range(2):
                c = cp * 2 + cc
                b = fm_matmul(W, Wt, cc * 128, yb, lambda kc: [T_yb[kc]], H)
                TT(fm_view(MM[2 + cc]), ps[:, b:b + 2, 0:HW], fm_view(MM[2 + cc]), ALU.mult,
                   [T_ps[b], T_ps[b + 1], TM[2 + cc]], [TM[2 + cc]])
                TT(merged[:, c, :], MM[cc], MM[2 + cc], ALU.add, [TM[cc], TM[2 + cc]], [T_mg[c]])
            wst.done(wi)

    def phase_O(row0, srow0):
        P.tag = "phase_O"
        DMA("sp", gpost_b[:], gpost_d.partition_broadcast(128), "c0", [], [T_gpost])
        nblk = D // 256
        for blk in range(nblk):
            Wap, Wt, wi = wst.get([w_o_d[:, blk * 256:(blk + 1) * 256]], KC)
            for j in range(NB + 1):
                rows = 128 if j < NB else NS
                col0 = j * 128 if j < NB else NP
                b = ps_tm()
                for kc in range(KC):
                    MM_(ps[0:rows, b, 0:256], merged[:, kc, col0:col0 + rows], Wap[:, kc, 0:256], kc == 0, kc == KC - 1,
                        [Wt[0], Wt[1], T_mg[kc]], [T_ps[b]], signal=(kc == KC - 1))
                dst = y_tm[:, j, blk * 256:(blk + 1) * 256] if j < NB else ys_tm[:, blk * 256:(blk + 1) * 256]
                Td = T_y[j] if j < NB else T_ys
                ACT(dst, ps[0:rows, b, 0:256], AF.Copy, [T_ps[b]], [Td])
                ACT(junkO[0:rows, :], ps[0:rows, b, 0:256], AF.Square, [T_ps[b]], [T_junkO, T_ssq[j]],
                    accum_out=ssq[0:rows, j * nblk + blk: j * nblk + blk + 1])
            wst.done(wi)
        ACT(small[:, 4:5], small[:, 5:6], AF.Copy, [], T_ssq + [T_small])
        switch("r3", T_xf)
        order = [NB] + list(range(NB))
        for n, j in enumerate(order):
            rows = 128 if j < NB else NS
            Y = y_tm[:, j, :] if j < NB else ys_tm[:, :]
            Td = T_y[j] if j < NB else T_ys
            if j < NB:
                xsrc = xp_d[row0 + j * 128: row0 + (j + 1) * 128, :]
                dstd = yp_d[row0 + j * 128: row0 + (j + 1) * 128, :]
            else:
                xsrc = xs_d[srow0:srow0 + NS, :]
                dstd = ys_d[srow0:srow0 + NS, :]
            xb, Txb = xf[n % 2], T_xf[n % 2]
            DMA("sp", xb[0:rows, :], xsrc, "xq%d" % (n % 2), [], [Txb])
            sc = small[0:rows, 96 + (n % 2): 97 + (n % 2)]
            Tsm = T_smx[n % 2]
            P.op("dve", lambda j=j, rows=rows, sc=sc: dve_e.tensor_reduce(
                out=sc, in_=ssq[0:rows, j * nblk:(j + 1) * nblk], axis=mybir.AxisListType.X, op=ALU.add),
                [T_ssq[j]], [Tsm])
            TS(sc, sc, 1.0 / D, EPS, ALU.mult, ALU.add, [Tsm], [Tsm], eng="pool")
            P.op("pool", lambda sc=sc, rows=rows: pool_e.tensor_tensor(out=sc, in0=sc, in1=mhalf[0:rows, 0:1], op=ALU.pow),
                 [Tsm, T_const], [Tsm])
            STT(Y, Y, sc, gpost_b[0:rows, :], ALU.mult, ALU.mult, [Td, Tsm, T_gpost], [Td])
            hD = D // 2
            for hh in range(2):
                cs = slice(hh * hD, (hh + 1) * hD)
                Tq = T_yq[j][hh]
                TT(Y[:, cs], Y[:, cs], xb[0:rows, cs], ALU.add, [Td, Txb], [Tq])
                DMA("sp", dstd[:, cs], Y[:, cs], "yout%d" % ((2 * n + hh) % 4), [Tq], [])

    def all_xT():
        return [t for row in T_xT for t in row]

    def all_yq():
        return [t for row in T_yq for t in row]

    def schedule():
        for k in rot:
            rot[k] = 0
        rot["nfm"], rot["ntm"] = 2, 4
        wst.reset()
        region["r2"] = [T_wsld]
        region["r3"] = []
        setup()
        for p in range(NPASS):
            switch("r2", T_xs[0:1] + [T_junk])
            switch("r3", T_xs[1:3])
            phase_X(xpre_d, p * NP, 0, False)
            switch("r2", BS[0].tiles)
            switch("r3", BS[1].tiles)
            phase_prefix(first=(p == 0))
            if getattr(C, "stop_after", None) == "prefix0":
                return
        if getattr(C, "stop_after", None) == "prefix":
            return
        for p in range(NPASS):
            switch("r2", T_xs[0:1] + [T_junk])
            switch("r3", T_xs[1:3])
            nvpre = min(2, E // 256)
            vpre = [wst.get([w_in_d[:, C.cV + blk * 256: C.cV + (blk + 1) * 256]], KC) for blk in range(nvpre)]

            def vhook(j, vpre=vpre):
                for blk, (Wap, Wt, wi) in enumerate(vpre):
                    vproj_prompt(Wap, Wt, blk, j)

            phase_X(xp_d, p * NP, p * NS, True, vhook)
            switch("r2", A_tiles)
            switch("r3", A3_tiles)
            phase_A(p * NS, p == NPASS - 1, vpre)
            switch("r2", BS[0].tiles)
            switch("r3", BS[1].tiles)
            phase_B(p * NS)
            if getattr(C, "stop_after", None) == "B%d" % p:
                return
            if p == NPASS - 1:
                DMA("sp", sp_d.rearrange("h d e -> d h e"), Sf[:], "spout", T_Sf, [])
            switch("r2", TM)
            switch("r3", T_mg)
            rot["nfm"] = 4
            phase_M()
            rot["nfm"], rot["ntm"] = 2, 8
            if getattr(C, "stop_after", None) == "M%d" % p:
                return
            switch("r2", O_tiles)
            P.inherit(T_y, all_xT() + T_ya + T_yb + T_g)
            phase_O(p * NP, p * NS)
            rot["nfm"], rot["ntm"] = 2, 4
            P.inherit(all_xT() + T_ya + T_yb + T_g, T_y + all_yq() + [T_ys])

    P.dry = True
    schedule()
    P.dry = False
    schedule()

    sem_names = ["s_" + e for e in Prog.ENG] + ["l_" + k for k in P.lanes]
    sems = {n: es.enter_context(nc.semaphore(n)) for n in sem_names}
    for lane, n in P.lanes.items():
        P.ops["sp"].append(("wait", "l_" + lane, 16 * n))
    for e in ["pe", "act", "dve", "pool"]:
        if P.cnt[e] > 0:
            P.ops["sp"].append(("wait", "s_" + e, P.cnt[e]))

    eng_map = {"pe": pe_e, "act": act_e, "dve": dve_e, "pool": pool_e, "sp": sp_e}

    def run_engine(name):
        e = eng_map[name]
        for item in P.ops[name]:
            if item[0] == "wait":
                e.wait_ge(sems[item[1]], item[2])
            elif item[0] == "op":
                inst = item[1]()
                if item[2] is not None:
                    inst.then_inc(sems[item[2]], 1)
            else:
                inst = item[1]()
                inst.then_inc(sems[item[2]], 16)

    with nc.allow_non_contiguous_dma(reason="small parameter vectors / state layouts"):
        with nc.Block() as block:
            @block.tensor
            def _(t):
                run_engine("pe")

            @block.scalar
            def _(t):
                run_engine("act")

            @block.vector
            def _(t):
                run_engine("dve")

            @block.gpsimd
            def _(t):
                run_engine("pool")

            @block.sync
            def _(t):
                run_engine("sp")
    es.close()
    return nc, P


def _consts(cfg):
    NP, NS = cfg.NP, cfg.NS
    ident = np.eye(128, dtype=np.float32)
    s = np.arange(128)[:, None]
    t = np.arange(128)[None, :]
    tri = (s <= t).astype(np.float32)
    mask01 = np.ones((128, NP), np.float32)
    mask01[:, ::64] = 0.0
    blk = ((s // 64 == t // 64) & (s <= t)).astype(np.float32)
    maskT = np.tile(blk, (1, NP // 128))
    sel = np.zeros((NS, NS, 128), np.float32)
    for j in range(NS):
        sel[j, j, :] = 1.0
    return dict(c_ident=ident, c_tri=tri, c_mask01=mask01, c_maskT=maskT, c_sel=sel.reshape(NS, NS * 128))


_CACHE = {}


def run(cfg, x_prompt, x_sample, state_hgrn, lb_logits, g_pre, w_in, ln_g, ln_b, w_s, b_s,
        g_onorm, w_pa, w_pb, w_o, g_post, trace=False):
    C = cfg
    f = lambda a: np.ascontiguousarray(np.asarray(a, dtype=np.float32))
    x_prompt, x_sample, state_hgrn = f(x_prompt), f(x_sample), f(state_hgrn)
    B, SEQ, D = x_prompt.shape
    NTOK = C.NPASS * C.NP
    assert SEQ == 2 * NTOK and B * 2 == C.NCORES
    NSAMP = C.NPASS * C.NS
    DB = x_sample.shape[0]
    assert DB == NSAMP * C.NCORES
    key = (C.D, C.NP, C.NS, C.NPASS)
    if key not in _CACHE:
        _CACHE[key] = build_program(C)
    nc, _ = _CACHE[key]
    shared = dict(w_in=f(w_in)[0], w_pa=f(w_pa)[0], w_pb=f(w_pb)[0], w_o=f(w_o)[0], lb_logits=f(lb_logits),
                  g_pre=f(g_pre)[0], ln_g=f(ln_g)[0], ln_b=f(ln_b)[0], w_s=f(w_s)[0], b_s=f(b_s)[0],
                  g_onorm=f(g_onorm)[0], g_post=f(g_post)[0])
    shared.update(_consts(C))
    xs2d = x_sample.reshape(DB, D)
    in_maps = []
    for c in range(C.NCORES):
        b, half = c // 2, c % 2
        m = dict(shared)
        m["xp"] = x_prompt[b, half * NTOK:(half + 1) * NTOK]
        m["xpre"] = x_prompt[b, 0:NTOK] if half == 1 else np.zeros((NTOK, D), np.float32)
        m["xs"] = xs2d[c * NSAMP:(c + 1) * NSAMP]
        m["st"] = state_hgrn[0, c * NSAMP:(c + 1) * NSAMP]
        in_maps.append(m)
    res = run_bass_kernel_spmd(nc, in_maps, core_ids=list(range(C.NCORES)), trace=trace)
    R = res.results
    yp = np.stack([np.concatenate([R[2 * b]["yp"], R[2 * b + 1]["yp"]], axis=0) for b in range(B)])
    ys = np.concatenate([R[c]["ys"] for c in range(C.NCORES)], axis=0).reshape(DB, 1, D)
    sp = np.stack([R[2 * b + 1]["sp"] for b in range(B)])[None]
    ss = np.concatenate([R[c]["ss"] for c in range(C.NCORES)], axis=0)[None]
    vp = np.stack([R[2 * b + 1]["vp"] for b in range(B)])[None]
    vs = np.concatenate([R[c]["vs"] for c in range(C.NCORES)], axis=0).reshape(DB, 1, C.E)[None]
    out = (yp.astype(np.float32), ys.astype(np.float32), sp.astype(np.float32), ss.astype(np.float32),
           vp.astype(np.float32), vs.astype(np.float32))
    return out, res


def kernel(x_prompt, x_sample, state_hgrn, lb_logits, g_pre, w_in, ln_g, ln_b, w_s, b_s,
           g_onorm, w_pa, w_pb, w_o, g_post):
    out, _ = run(Cfg(), x_prompt, x_sample, state_hgrn, lb_logits, g_pre, w_in, ln_g, ln_b, w_s, b_s,
                 g_onorm, w_pa, w_pb, w_o, g_post)
    return out
```

```python
import numpy as np
import concourse.bass as bass
import concourse.mybir as mybir
from concourse.bass_utils import run_bass_kernel_spmd

F32 = mybir.dt.float32
BF16 = mybir.dt.bfloat16
AF = mybir.ActivationFunctionType
ALU = mybir.AluOpType
EPS = 1e-6
GELU_C = 0.044715
GELU_S = 1.5957691216057308


class Cfg:
    def __init__(self, D=4096, NP=512, NS=8, NPASS=2, NCORES=8):
        self.D = D
        self.KC = D // 128
        self.E = D // 2
        self.H = self.E // 128
        self.NP, self.NS, self.NPASS, self.NCORES = NP, NS, NPASS, NCORES
        self.NT = NP + NS
        self.HW = self.NT // 2
        self.NB = NP // 128
        self.NCH = NP // 64
        self.NCOLS = 5 * D + D // 2
        E = self.E
        self.cU, self.cV, self.cZA, self.cQ, self.cF, self.cI, self.cZB = [i * E for i in range(7)]
        self.cGA = 7 * E
        self.cGB = 7 * E + D
        self.WB = 256


class Tile:
    __slots__ = ("name", "w", "r", "also", "excl")

    def __init__(self, name, also=(), excl=False):
        self.name = name
        self.excl = excl
        self.w = None
        self.r = {}
        self.also = list(also)


class Prog:
    ENG = ["pe", "act", "dve", "pool", "sp"]

    def __init__(self):
        self.ops = {e: [] for e in self.ENG}
        self.cnt = {e: 0 for e in self.ENG}
        self.known = {e: {} for e in self.ENG}
        self.lanes = {}
        self.dry = False
        self.tag = ""

    def _waits(self, eng, reads, writes):
        need = {}

        def add(tok):
            if tok is None:
                return
            s, v = tok
            if need.get(s, 0) < v:
                need[s] = v

        for t in reads:
            add(t.w)
        for t in writes:
            add(t.w)
            for s, v in t.r.items():
                add((s, v))
        out = []
        kn = self.known[eng]
        for s, v in need.items():
            if eng == "pe" and s == "s_pe":
                continue
            if kn.get(s, 0) >= v:
                continue
            kn[s] = v
            out.append((s, v))
        return out

    def _mark(self, tok, reads, writes):
        s, v = tok
        for t in reads:
            if t.r.get(s, 0) < v:
                t.r[s] = v
        for t in writes:
            t.w = tok
            t.r = {}

    @staticmethod
    def _expand(writes):
        out = list(writes)
        for t in writes:
            out.extend(t.also)
        return out

    def op(self, eng, fn, reads=(), writes=(), signal=True):
        if self.dry:
            return
        writes = self._expand(writes) + [t for t in reads if t.excl]
        reads = [t for t in reads if not t.excl]
        waits = self._waits(eng, reads, writes)
        if signal:
            self.cnt[eng] += 1
            tok = ("s_" + eng, self.cnt[eng])
        else:
            tok = ("s_" + eng, self.cnt[eng] + 1)
        self._mark(tok, reads, writes)
        lst = self.ops[eng]
        for w in waits:
            lst.append(("wait", w[0], w[1]))
        lst.append(("op", fn, ("s_" + eng) if signal else None, self.tag))

    def dma(self, q, fn, lane, reads=(), writes=()):
        if self.dry:
            return
        writes = self._expand(writes)
        waits = self._waits(q, reads, writes)
        n = self.lanes.get(lane, 0)
        sem = "l_" + lane
        if n > 0 and self.known[q].get(sem, 0) < 16 * n:
            self.known[q][sem] = 16 * n
            waits.append((sem, 16 * n))
        self.lanes[lane] = n + 1
        tok = (sem, 16 * (n + 1))
        self._mark(tok, reads, writes)
        lst = self.ops[q]
        for w in waits:
            lst.append(("wait", w[0], w[1]))
        lst.append(("dma", fn, sem))

    def inherit(self, new_tiles, old_tiles):
        if self.dry:
            return
        acc = {}
        for t in old_tiles:
            if t.w is not None:
                s, v = t.w
                acc[s] = max(acc.get(s, 0), v)
            for s, v in t.r.items():
                acc[s] = max(acc.get(s, 0), v)
        for t in new_tiles:
            t.w = None
            t.r = dict(acc)


def build_program(cfg):
    C = cfg
    D, KC, E, H, NP, NS, NT, HW, NB, NCH, WB = C.D, C.KC, C.E, C.H, C.NP, C.NS, C.NT, C.HW, C.NB, C.NCH, C.WB
    NPASS = C.NPASS
    NTOK = NPASS * NP
    NSAMP = NPASS * NS
    nc = bass.Bass("TRN2", target_bir_lowering=False)
    P = Prog()

    def din(name, shape):
        return nc.dram_tensor(name, list(shape), F32, kind="ExternalInput").ap()

    def dout(name, shape):
        return nc.dram_tensor(name, list(shape), F32, kind="ExternalOutput").ap()

    xp_d = din("xp", [NTOK, D])
    xpre_d = din("xpre", [NTOK, D])
    xs_d = din("xs", [NSAMP, D])
    st_d = din("st", [NSAMP, H, 128, 128])
    w_in_d = din("w_in", [D, C.NCOLS])
    w_pa_d = din("w_pa", [E, D])
    w_pb_d = din("w_pb", [E, D])
    w_o_d = din("w_o", [D, D])
    lbl_d = din("lb_logits", [2, E])
    gpre_d = din("g_pre", [D])
    lng_d = din("ln_g", [E])
    lnb_d = din("ln_b", [E])
    ws_d = din("w_s", [H, 128, 128])
    bs_d = din("b_s", [H, 128])
    gon_d = din("g_onorm", [E])
    gpost_d = din("g_post", [D])
    c_ident_d = din("c_ident", [128, 128])
    c_tri_d = din("c_tri", [128, 128])
    c_mask01_d = din("c_mask01", [128, NP])
    c_maskT_d = din("c_maskT", [128, NP])
    c_sel_d = din("c_sel", [NS, NS * 128])

    yp_d = dout("yp", [NTOK, D])
    ys_d = dout("ys", [NSAMP, D])
    sp_d = dout("sp", [H, 128, 128])
    ss_d = dout("ss", [NSAMP, H, 128, 128])
    vp_d = dout("vp", [128, E])
    vs_d = dout("vs", [NSAMP, E])

    R1_B = KC * NT * 2 + 2 * H * NT * 2
    Y_B = NB * D * 4
    R1_B = max(R1_B, Y_B)
    R3_B = max(KC * NT * 2, 2 * D * 4, 2 * E * 4 + H * 128 * 4 + E * 4)
    WS_B = KC * WB * 2
    NWS = 3
    BSET_B = 8 * NT * 4 + 4 * NP * 2 + 2 * NB * 128 * 2 + 128 * 4 + NS * 128 * 4 + NCH * 128 * 2 + 128 * 4 + NT * 2 + NS * 4
    R3_B = max(R3_B, BSET_B)
    R2_B = max(
        BSET_B,
        D * 4 + D * 2,
        8 * NT * 4 + 4 * NP * 2 + NB * 128 * 2 + NB * 256 * 2 + 256 * 4 + 256 * 2 + NS * 128 * 4 + 64,
        NB * E * 2 + E * 2 + 6 * NT * 4,
        D * 4 + D * 4 + D + 64,
    )
    R2_B = (R2_B + 63) // 64 * 64

    import contextlib
    es = contextlib.ExitStack()

    def sb(name, shape, dt):
        return es.enter_context(nc.sbuf_tensor(name, list(shape), dt))

    R1 = sb("R1", [128, R1_B // 4], F32)
    R2 = sb("R2", [128, R2_B // 4], F32)
    R3 = sb("R3", [128, R3_B // 4], F32)
    WS = [sb("WS%d" % i, [128, KC, WB], BF16) for i in range(NWS)]
    Sf = sb("Sf", [128, H, 128], F32)
    Sb = sb("Sb", [128, H, 128], BF16)
    Sbprev = Sb
    identf = sb("identf", [128, 128], F32)
    identb = sb("identb", [128, 128], BF16)
    trif = sb("trif", [128, 128], F32)
    mask01 = sb("mask01", [128, NP], BF16)
    maskT = sb("maskT", [128, NP], BF16)
    mstage = None
    onesf = sb("onesf", [128, 128], F32)
    sel = sb("sel", [NS, NS * 128], F32)
    gpreT = sb("gpreT", [128, KC], F32)
    lbT = sb("lbT", [128, H], F32)
    omlT = sb("omlT", [128, H], F32)
    lb1T = sb("lb1T", [128, H], F32)
    gonT = sb("gonT", [128, H], F32)
    wmT = sb("wmT", [128, H, 128], BF16)
    Rsm = sb("Rsm", [NS, H, NS], BF16)
    w00 = sb("w00", [NS, H], F32)
    bs0 = sb("bs0", [128, H], F32)
    small = sb("small", [128, 128], F32)
    ssq = sb("ssq", [128, (NB + 1) * (D // 256)], F32)
    junkO = sb("junkO", [128, 256], BF16)
    epsb = sb("epsb", [128, 1], F32)
    mhalf = sb("mhalf", [128, 1], F32)

    ps = es.enter_context(nc.psum_tensor("ps", [128, 8, 512], F32))

    def view(R, off, shape, dt, parts=128):
        esz = 2 if dt == BF16 else 4
        n = int(np.prod(shape[1:]))
        nbytes = n * esz
        assert off % 4 == 0 and nbytes % 4 == 0
        ap = R[0:parts, off // 4:(off + nbytes) // 4]
        if dt != F32:
            ap = ap.bitcast(dt)
        if len(shape) == 3:
            ap = ap.rearrange("p (a b) -> p a b", a=shape[1])
        return ap

    xT = view(R1, 0, [128, KC, NT], BF16)
    ya = view(R1, KC * NT * 2, [128, H, NT], BF16)
    yb = view(R1, KC * NT * 2 + H * NT * 2, [128, H, NT], BF16)
    g_tm = view(R1, KC * NT * 2, [128, NB, E], F32) if NB * E * 4 <= 2 * H * NT * 2 else None
    assert g_tm is not None
    y_tm = view(R1, 0, [128, NB, D], F32)
    wsld2 = view(R1, KC * NT * 2, [128, 128], F32)
    wsall = view(R1, KC * NT * 2, [128, H, 128], F32)
    merged = view(R3, 0, [128, KC, NT], BF16)
    xs2 = view(R3, 0, [128, D], F32)
    xs3 = view(R3, D * 4, [128, D], F32) if 2 * D * 4 <= R3_B else None
    xf = [view(R3, 0, [128, D], F32), view(R3, D * 4, [128, D], F32)]
    lng_b = view(R3, 0, [128, E], F32)
    lnb_b = view(R3, E * 4, [128, E], F32)
    bsb = view(R3, 2 * E * 4, [128, H, 128], F32)
    gs = view(R3, 2 * E * 4 + H * 128 * 4, [NS, E], F32, parts=NS)
    wsld = view(R2, 0, [128, 128], F32)
    mst = view(R2, 512, [128, NP], F32)
    xs1 = view(R2, 0, [128, D], F32)
    junk = view(R2, D * 4, [128, D], BF16)
    class BSet:
        def __init__(self, R, RB, tag):
            o = 0
            self.A = []
            for i in range(8):
                self.A.append(view(R, o, [128, NT], F32)); o += NT * 4
            self.qb = view(R, o, [128, NP], BF16); o += NP * 2
            self.kb = view(R, o, [128, NP], BF16); o += NP * 2
            self.kdec = view(R, o, [128, NP], BF16); o += NP * 2
            self.att_sb = view(R, o, [128, NP], BF16); o += NP * 2
            self.kdec_tm = view(R, o, [128, NB, 128], BF16); o += NB * 128 * 2
            self.i_tm = view(R, o, [128, NB, 128], BF16); o += NB * 128 * 2
            self.i_s = view(R, o, [NS, 128], F32, parts=NS); o += 128 * 4
            self.Sin = view(R, o, [128, NS, 128], F32); o += NS * 128 * 4
            self.Sball = view(R, o, [128, NCH, 128], BF16); o += NCH * 128 * 2
            self.Sg = view(R, o, [128, 128], F32); o += 128 * 4
            self.iTb = view(R, o, [128, NT], BF16); o += NT * 2
            self.iTs = view(R, o, [128, NS], F32); o += NS * 4
            assert o <= RB, (o, RB)
            self.TA = [Tile("A%d%s" % (i, tag)) for i in range(8)]
            self.T_qb, self.T_kb, self.T_kdec, self.T_att = Tile("qb"), Tile("kb"), Tile("kdec"), Tile("att")
            self.T_kdtm = [Tile("kdtm") for _ in range(NB)]
            self.T_itm = [Tile("itm") for _ in range(NB)]
            self.T_is, self.T_Sin = Tile("is"), Tile("Sin")
            self.T_Sball = [Tile("sball") for _ in range(NCH)]
            self.T_Sg = Tile("Sg")
            self.T_iTb, self.T_iTs = Tile("iTb"), Tile("iTs")
            self.tiles = (self.TA + [self.T_qb, self.T_kb, self.T_kdec, self.T_att, self.T_is, self.T_Sin]
                          + self.T_kdtm + self.T_itm + self.T_Sball + [self.T_Sg, self.T_iTb, self.T_iTs])

    o = 0
    vn_bf = view(R2, o, [128, NB, E], BF16); o += NB * E * 2
    vns_bf = view(R2, o, [NS, E], BF16, parts=NS); o += E * 2
    AA = []
    for i in range(6):
        AA.append(view(R2, o, [128, NT], F32)); o += NT * 4
    assert o <= R2_B
    MM = [view(R2, i * NT * 4, [128, NT], F32) for i in range(4)]
    gpost_b = view(R2, 0, [128, D], F32)
    ys_tm = view(R2, D * 4, [NS, D], F32, parts=NS)
    xq = [view(R2, 2 * D * 4, [128, D // 8], F32), view(R2, 2 * D * 4 + D // 2, [128, D // 8], F32)]

    T_xT = [[Tile("xT") for _ in range(NB + 1)] for _ in range(KC)]
    T_ya = [Tile("ya") for _ in range(H)]
    T_yb = [Tile("yb") for _ in range(H)]
    T_g = [Tile("g", also=T_ya + T_yb) for _ in range(NB)]
    for t in T_ya + T_yb:
        t.also = list(T_g)
    T_y = [Tile("y") for _ in range(NB)]
    T_mg = [Tile("merged") for _ in range(KC)]
    T_WS = [(Tile("wsa"), Tile("wsb")) for _ in range(NWS)]
    T_ps = [Tile("psb%d" % b, excl=True) for b in range(8)]
    T_q7 = [T_ps[4], T_ps[4], T_ps[4], T_ps[4]]
    T_Sf = [Tile("Sf") for _ in range(H)]
    T_Sb = [Tile("Sb") for _ in range(H)]
    T_Sbprev = T_Sb
    T_const = Tile("const")
    T_small = Tile("small")
    T_smx = [Tile("smx0"), Tile("smx1")]
    BS = [BSet(R2, R2_B, "a"), BSet(R3, R3_B, "b")]
    TAA = [Tile("AA%d" % i) for i in range(6)]
    T_vn = [Tile("vn") for _ in range(NB)]
    T_vns = Tile("vns")
    A_tiles = TAA + T_vn + [T_vns]
    T_lnc, T_bsb, T_gs = Tile("lnc"), Tile("bsb"), Tile("gs")
    A3_tiles = [T_lnc, T_bsb, T_gs]
    TM = [Tile("M%d" % i) for i in range(4)]
    T_gpost, T_ys, T_junkO = Tile("gpost"), Tile("ys"), Tile("junkO")
    T_xq = [Tile("xq0"), Tile("xq1")]
    T_xf = [Tile("xf0"), Tile("xf1")]
    T_ssq = [Tile("ssq") for _ in range(NB + 1)]
    T_yq = [[Tile("yq") for _ in range(4)] for _ in range(NB + 1)]
    O_tiles = [T_gpost, T_ys] + T_xq
    T_xs = [Tile("xs1"), Tile("xs2"), Tile("xs3")]
    T_junk = Tile("junk")
    T_wsld = Tile("wsld")
    T_wsld2 = Tile("wsld2")
    for _t in T_g:
        _t.also.append(T_wsld2)
    T_wm = [Tile("wm") for _ in range(H)]
    region = {"r2": [T_wsld], "r3": []}

    def switch(reg, new_tiles):
        P.inherit(new_tiles, region[reg])
        region[reg] = list(new_tiles)

    sp_e, act_e, dve_e, pe_e, pool_e = nc.sync, nc.scalar, nc.vector, nc.tensor, nc.gpsimd

    def ACT(out, in_, func, reads, writes, **kw):
        P.op("act", lambda: act_e.activation(out=out, in_=in_, func=func, **kw), reads, writes)

    def TS(out, in0, s1, s2, op0, op1, reads, writes, eng="dve"):
        e = dve_e if eng == "dve" else pool_e
        if op1 is None:
            P.op(eng, lambda: e.tensor_scalar(out=out, in0=in0, scalar1=s1, scalar2=None, op0=op0), reads, writes)
        else:
            P.op(eng, lambda: e.tensor_scalar(out=out, in0=in0, scalar1=s1, scalar2=s2, op0=op0, op1=op1), reads, writes)

    def TT(out, in0, in1, op, reads, writes, eng="dve"):
        e = dve_e if eng == "dve" else pool_e
        P.op(eng, lambda: e.tensor_tensor(out=out, in0=in0, in1=in1, op=op), reads, writes)

    def STT(out, in0, scalar, in1, op0, op1, reads, writes):
        P.op("dve", lambda: dve_e.scalar_tensor_tensor(out=out, in0=in0, scalar=scalar, in1=in1, op0=op0, op1=op1), reads, writes)

    def CP(out, in_, reads, writes, eng="dve"):
        e = dve_e if eng == "dve" else pool_e
        P.op(eng, lambda: e.tensor_copy(out=out, in_=in_), reads, writes)

    def MM_(out, lhsT, rhs, start, stop, reads, writes, signal):
        old = P.tag
        if lhsT.dtype == F32:
            P.tag = old + "#fp32"
        P.op("pe", lambda: pe_e.matmul(out, lhsT, rhs, start=start, stop=stop), reads, writes, signal=signal)
        P.tag = old

    def TR(out, in_, ident, reads, writes, signal=True):
        P.op("pe", lambda: pe_e.transpose(out, in_, ident), reads, writes, signal=signal)

    def DMA(q, out, in_, lane, reads, writes, **kw):
        e = {"sp": sp_e, "pool": pool_e, "act": act_e}[q]
        P.dma(q, lambda: e.dma_start(out=out, in_=in_, **kw), lane, reads, writes)

    class WStream:
        def __init__(self):
            self.specs = []
            self.issued = 0
            self.n = 0
            self.done_set = set()

        def reset(self):
            self.n = 0
            self.issued = 0
            self.done_set = set()

        def get(self, segs, kc):
            i = self.n
            self.n += 1
            if P.dry:
                self.specs.append((segs, kc))
                return WS[i % NWS], T_WS[i % NWS], i
            self._pump()
            assert self.issued > i, "weight block %d requested before its slot was released" % i
            return WS[i % NWS], T_WS[i % NWS], i

        def done(self, i):
            if P.dry:
                return
            self.done_set.add(i)
            self._pump()

        def _pump(self):
            while self.issued < len(self.specs) and (self.issued < NWS or (self.issued - NWS) in self.done_set):
                self._issue(self.issued)
                self.issued += 1

        def _issue(self, j):
            segs, kc = self.specs[j]
            slot = j % NWS
            c0 = 0
            for sg in segs:
                ncols = sg.shape[1]
                src = sg.rearrange("(kc p) n -> p kc n", p=128)
                if len(segs) == 1:
                    DMA("pool", WS[slot][:, 0:kc, c0:c0 + ncols], src, "ws%da" % slot, [], list(T_WS[slot]))
                else:
                    assert len(segs) == 2 and ncols == 128
                    k = c0 // 128
                    DMA("pool", WS[slot][:, 0:kc, c0:c0 + ncols], src, "ws%d%s" % (slot, "ab"[k]), [], [T_WS[slot][k]])
                c0 += ncols

    wst = WStream()

    rot = {"fm": 0, "tm": 0, "o": 0, "q7": 0, "nfm": 2, "ntm": 4}

    def ps_fm():
        k = rot["fm"] % rot["nfm"]; rot["fm"] = k + 1
        return 2 * k

    def ps_tm():
        k = rot["tm"] % rot["ntm"]; rot["tm"] = k + 1
        return k

    def ps_q7():
        k = rot["q7"]; rot["q7"] = (k + 1) % 3
        return k

    def fm_view(sbuf2d):
        return sbuf2d.rearrange("p (a b) -> p a b", a=2)

    def fm_matmul(Wap, Wt, c0, src, src_tiles, kcn):
        b = ps_fm()
        for kc in range(kcn):
            for hf in range(2):
                MM_(ps[:, b + hf, 0:HW], Wap[:, kc, c0:c0 + 128], src[:, kc, hf * HW:(hf + 1) * HW],
                    kc == 0, kc == kcn - 1, [Wt[0] if c0 < 128 else Wt[1]] + src_tiles(kc), [T_ps[b + hf]],
                    signal=(kc == kcn - 1 and hf == 1))
        return b

    def xT_tiles(kc):
        return T_xT[kc]

    def setup():
        P.tag = "setup"
        DMA("act", identf[:], c_ident_d, "cs1", [], [T_const])
        DMA("act", trif[:], c_tri_d, "cs2", [], [T_const])
        DMA("act", mst[:], c_mask01_d, "c0", [], [T_wsld])
        CP(mask01[:], mst[:], [T_wsld], [T_const])
        DMA("act", mst[:], c_maskT_d, "c0", [], [T_wsld])
        CP(maskT[:], mst[:], [T_wsld], [T_const])
        DMA("act", sel[:], c_sel_d, "cs3", [], [T_const])
        DMA("act", gpreT[:], gpre_d.rearrange("(k p) -> p k", p=128), "cs0", [], [T_const])
        DMA("act", lbT[:], lbl_d[0, :].rearrange("(h p) -> p h", p=128), "cs1", [], [T_const])
        DMA("act", lb1T[:], lbl_d[1, :].rearrange("(h p) -> p h", p=128), "cs2", [], [T_const])
        DMA("act", gonT[:], gon_d.rearrange("(h p) -> p h", p=128), "cs3", [], [T_const])
        DMA("act", bs0[:], bs_d[:, 0].partition_broadcast(128), "cs0", [], [T_const])
        DMA("act", w00[:], ws_d[:, 0, 0].partition_broadcast(NS), "cs1", [], [T_const])
        P.op("dve", lambda: dve_e.memset(onesf[:], 1.0), [], [T_const])
        P.op("dve", lambda: dve_e.memset(epsb[:], EPS), [], [T_const])
        P.op("dve", lambda: dve_e.memset(mhalf[:], -0.5), [], [T_const])
        P.op("dve", lambda: dve_e.memset(small[:], 0.0), [], [T_small])
        CP(identb[:], identf[:], [T_const], [T_const])
        TT(lbT[:], lbT[:], lb1T[:], ALU.subtract, [T_const], [T_const])
        ACT(lbT[:], lbT[:], AF.Sigmoid, [T_const], [T_const])
        TS(omlT[:], lbT[:], -1.0, 1.0, ALU.mult, ALU.add, [T_const], [T_const])
        for g in range(H):
            TS(Rsm[:, g, :], identf[0:NS, 0:NS], w00[:, g:g + 1], None, ALU.mult, None, [T_const], [T_const])
        DMA("pool", wsall[:, :, :], ws_d.rearrange("g t s -> t g s"), "c1", [], [T_wsld2])
        G4 = min(4, H)
        for g4 in range(H // G4):
            bk = ps_tm()
            for k in range(G4):
                TR(ps[:, bk, k * 128:(k + 1) * 128], wsall[:, g4 * G4 + k, :], identf[:], [T_wsld2, T_const], [T_ps[bk]],
                   signal=(k == G4 - 1))
            TT(wmT[:, g4 * G4:(g4 + 1) * G4, :], ps[:, bk, 0:G4 * 128].rearrange("p (a b) -> p a b", a=G4),
               trif[:].unsqueeze(1).broadcast_to([128, G4, 128]), ALU.mult, [T_ps[bk], T_const],
               [T_wm[g] for g in range(g4 * G4, (g4 + 1) * G4)])
        P.op("dve", lambda: dve_e.memset(xT[:, :, NP:NT], 0.0), [], [T_xT[kc][NB] for kc in range(KC)])
        P.op("dve", lambda: dve_e.memset(Sf[:], 0.0), [], T_Sf)
        P.op("dve", lambda: dve_e.memset(Sb[:], 0.0), [], T_Sb)

    def phase_X(x_dram, row0, srow0, with_samples, vhook=None):
        P.tag = "phase_X"
        bufs = [xs1, xs2] + ([xs3] if xs3 is not None else [])
        nbuf = len(bufs)
        tiles = [(x_dram[row0 + j * 128: row0 + (j + 1) * 128, :], 128, j) for j in range(NB)]
        if with_samples:
            tiles.append((xs_d[srow0: srow0 + NS, :], NS, NB))
        for n, (src, rows, j) in enumerate(tiles):
            xb, Tx = bufs[n % nbuf], T_xs[n % nbuf]
            DMA("sp", xb[0:rows, :], src, "x%d" % (n % nbuf), [], [Tx])
            sa = small[0:rows, 80 + 2 * (n % 2): 81 + 2 * (n % 2)]
            sc = small[0:rows, 81 + 2 * (n % 2): 82 + 2 * (n % 2)]
            Tsm = T_smx[n % 2]
            ACT(junk[0:rows, :], xb[0:rows, :], AF.Square, [Tx], [T_junk, Tsm], accum_out=sa)
            ACT(sc, sa, AF.Copy, [Tsm], [Tsm])
            TS(sc, sc, 1.0 / D, EPS, ALU.mult, ALU.add, [Tsm], [Tsm], eng="pool")
            P.op("pool", lambda sc=sc, rows=rows: pool_e.tensor_tensor(out=sc, in0=sc, in1=mhalf[0:rows, 0:1], op=ALU.pow),
                 [Tsm, T_const], [Tsm])
            TS(xb[0:rows, :], xb[0:rows, :], sc, 0.0, ALU.mult, ALU.add, [Tx, Tsm], [Tx], eng="pool")
            col0 = j * 128 if rows == 128 else NP
            for k4 in range(KC // 4):
                b = ps_tm()
                for kk in range(4):
                    kc = k4 * 4 + kk
                    TR(ps[:, b, kk * 128: kk * 128 + rows], xb[0:rows, kc * 128:(kc + 1) * 128],
                       identf[0:rows, 0:rows], [Tx, T_const], [T_ps[b]], signal=(kk == 3))
                src_ps = ps[:, b, :].rearrange("p (a b) -> p a b", a=4)[:, :, 0:rows]
                gp = gpreT[:, k4 * 4:(k4 + 1) * 4].unsqueeze(2).broadcast_to([128, 4, rows])
                TT(xT[:, k4 * 4:(k4 + 1) * 4, col0:col0 + rows], src_ps, gp, ALU.mult,
                   [T_ps[b], T_const], [T_xT[kc][j] for kc in range(k4 * 4, k4 * 4 + 4)])
            if vhook is not None and n >= 1 and tiles[n - 1][1] == 128:
                vhook(tiles[n - 1][2])
                P.tag = "phase_X"
        if vhook is not None and tiles[-1][1] == 128:
            vhook(tiles[-1][2])

    BK_D0, BK_D1, BK_ATT, BK_O = 4, 5, 6, 7

    def proj_i(S_, Wap, Wt, with_samples, ic0):
        bI = fm_matmul(Wap, Wt, ic0, xT, xT_tiles, KC)
        TpI = [T_ps[bI], T_ps[bI + 1]]
        ACT(fm_view(S_.iTb), ps[:, bI:bI + 2, 0:HW], AF.Copy, TpI, [S_.T_iTb])
        if with_samples:
            ACT(S_.iTs[:, :], ps[:, bI + 1, NP - HW:NT - HW], AF.Copy, TpI, [S_.T_iTs])

    def proj_i_tr(S_, with_samples):
        P.tag = "stage_P1"
        b = BK_O
        for j in range(NB):
            dst = ps[:, b, j * 64:(j + 1) * 64].bitcast(BF16)[:, 0:128]
            TR(dst, S_.iTb[:, j * 128:(j + 1) * 128], identb[:], [S_.T_iTb, T_const], [T_ps[b]],
               signal=(j == NB - 1 and not with_samples))
        if with_samples:
            TR(ps[0:NS, b, 256:384], S_.iTs[:, :], identf[:], [S_.T_iTs, T_const], [T_ps[b]])
        dsta = ps[:, b, 0:NB * 64].bitcast(BF16).rearrange("p (j d) -> p j d", d=128)
        ACT(S_.i_tm[:, :, :], dsta, AF.Copy, [T_ps[b]], S_.T_itm)
        if with_samples:
            ACT(S_.i_s[:, :], ps[0:NS, b, 256:384], AF.Copy, [T_ps[b]], [S_.T_is])

    def gate_math_a(S_, h, bF):
        A, TA = S_.A, S_.TA
        Tp = [T_ps[bF], T_ps[bF + 1]]
        ACT(fm_view(A[1]), ps[:, bF:bF + 2, 0:HW], AF.Sigmoid, Tp, [TA[1]])
        ACT(A[2], A[1], AF.Ln, [TA[1], T_const], [TA[2]], scale=omlT[:, h:h + 1], bias=lbT[:, h:h + 1])
        P.op("dve", lambda: dve_e.tensor_tensor_scan(out=A[3][:, 0:NP], data0=mask01[:], data1=A[2][:, 0:NP],
                                                      initial=0.0, op0=ALU.mult, op1=ALU.add),
             [TA[2], T_const], [TA[3]])
        TS(A[1], A[1], omlT[:, h:h + 1], lbT[:, h:h + 1], ALU.mult, ALU.add, [TA[1], T_const], [TA[1]])
        TS(A[7], A[1], -1.0, 1.0, ALU.mult, ALU.add, [TA[1]], [TA[7]])

    def gate_math_b(S_, h, full):
        A, TA = S_.A, S_.TA
        ACT(A[5][:, 0:NP], A[3][:, 0:NP], AF.Exp, [TA[3]], [TA[5]])
        ACT(A[6][:, 0:NP], A[3][:, 0:NP], AF.Exp, [TA[3]], [TA[6]], scale=-1.0)
        TT(A[6][:, 0:NP], A[7][:, 0:NP], A[6][:, 0:NP], ALU.mult, [TA[7], TA[6]], [TA[6]])
        ebl = A[5][:, 0:NP].rearrange("p (c s) -> p c s", s=64)[:, :, 63:64].broadcast_to([128, NCH, 64])
        TT(S_.kdec.rearrange("p (c s) -> p c s", s=64), A[6][:, 0:NP].rearrange("p (c s) -> p c s", s=64), ebl,
           ALU.mult, [TA[6], TA[5]], [S_.T_kdec])
        if full:
            CP(S_.kb[:], A[6][:, 0:NP], [TA[6]], [S_.T_kb])

    def stage_P1a(S_, h, srow0, prefix):
        P.tag = "stage_P1"
        if prefix:
            W, Wt, wi = wst.get([w_in_d[:, C.cF + h * 128: C.cF + (h + 1) * 128],
                                 w_in_d[:, C.cI + h * 128: C.cI + (h + 1) * 128]], KC)
            S_.wblocks = [(W, Wt, wi)]
            proj_i(S_, W, Wt, False, 128)
            return
        W1, W1t, w1 = wst.get([w_in_d[:, C.cI + h * 128: C.cI + (h + 1) * 128],
                               w_in_d[:, C.cQ + h * 128: C.cQ + (h + 1) * 128]], KC)
        S_.wblocks = [(W1, W1t, w1)]
        DMA("sp", S_.Sin[:], st_d[srow0:srow0 + NS, h].rearrange("s d e -> d s e"), "sin", [], [S_.T_Sin])
        proj_i(S_, W1, W1t, True, 0)

    def stage_P1b(S_, h, prefix):
        P.tag = "stage_P1"
        if prefix:
            return
        (W1, W1t, w1) = S_.wblocks[0]
        bQ = fm_matmul(W1, W1t, 128, xT, xT_tiles, KC)
        wst.done(w1)
        ACT(fm_view(S_.A[0]), ps[:, bQ:bQ + 2, 0:HW], AF.Silu, [T_ps[bQ], T_ps[bQ + 1]], [S_.TA[0]])

    def stage_P2(S_, h, prefix):
        P.tag = "stage_P2"
        if prefix:
            W, Wt, wi = S_.wblocks[0]
            bF = fm_matmul(W, Wt, 0, xT, xT_tiles, KC)
            wst.done(wi)
            gate_math_a(S_, h, bF)
            gate_math_b(S_, h, False)
            return
        W2, W2t, w2 = wst.get([w_in_d[:, C.cF + h * 128: C.cF + (h + 1) * 128],
                               w_in_d[:, C.cZB + h * 128: C.cZB + (h + 1) * 128]], KC)
        bF = fm_matmul(W2, W2t, 0, xT, xT_tiles, KC)
        gate_math_a(S_, h, bF)
        bZ = fm_matmul(W2, W2t, 128, xT, xT_tiles, KC)
        wst.done(w2)
        ACT(fm_view(S_.A[4]), ps[:, bZ:bZ + 2, 0:HW], AF.Silu, [T_ps[bZ], T_ps[bZ + 1]], [S_.TA[4]])
        gate_math_b(S_, h, True)
        TT(S_.qb[:], S_.A[0][:, 0:NP], S_.A[5][:, 0:NP], ALU.mult, [S_.TA[0], S_.TA[5]], [S_.T_qb])

    def stage_TR(S_, h):
        P.tag = "stage_B2"
        for j in range(NB):
            dst = ps[:, BK_ATT, j * 64:(j + 1) * 64].bitcast(BF16)[:, 0:128]
            TR(dst, S_.kdec[:, j * 128:(j + 1) * 128], identb[:], [S_.T_kdec, T_const], [T_ps[BK_ATT]], signal=(j == NB - 1))
        dsta = ps[:, BK_ATT, 0:NB * 64].bitcast(BF16).rearrange("p (j d) -> p j d", d=128)
        ACT(S_.kdec_tm[:, :, :], dsta, AF.Copy, [T_ps[BK_ATT]], S_.T_kdtm)

    def stage_B2(S_, h, prefix):
        P.tag = "stage_B2"
        A, TA = S_.A, S_.TA
        if not prefix:
            for j in range(NB):
                MM_(ps[:, BK_ATT, j * 128:(j + 1) * 128], S_.kb[:, j * 128:(j + 1) * 128], S_.qb[:, j * 128:(j + 1) * 128],
                    True, True, [S_.T_kb, S_.T_qb], [T_ps[BK_ATT]], signal=(j == NB - 1))
            TT(S_.att_sb[:], ps[:, BK_ATT, 0:NP], maskT[:], ALU.mult, [T_ps[BK_ATT], T_const], [S_.T_att])
        assert NCH <= 8
        for c in range(NCH):
            j, half = c // 2, c % 2
            pr = slice(64 * half, 64 * half + 64)
            bank = BK_D0 if half == 0 else BK_D1
            MM_(ps[:, bank, j * 128:(j + 1) * 128], S_.kdec_tm[pr, j, :], S_.i_tm[pr, j, :], True, True,
                [S_.T_kdtm[j], S_.T_itm[j]], [T_ps[bank]], signal=(c == NCH - 1))
        assert NCH % 2 == 0
        for c in range(NCH):
            j, half = c // 2, c % 2
            bank = BK_D0 if half == 0 else BK_D1
            if c % 2 == 0:
                src, Tsrc, dst, Tdst = Sf[:, h, :], T_Sf[h], S_.Sg[:, :], S_.T_Sg
            else:
                src, Tsrc, dst, Tdst = S_.Sg[:, :], S_.T_Sg, Sf[:, h, :], T_Sf[h]
            STT(dst, src, A[5][:, c * 64 + 63: c * 64 + 64], ps[:, bank, j * 128:(j + 1) * 128],
                ALU.mult, ALU.add, [Tsrc, TA[5], T_ps[bank]], [Tdst])
            if not prefix:
                ACT(S_.Sball[:, c, :], dst, AF.Copy, [Tdst], [S_.T_Sball[c]])
            elif c == NCH - 1:
                ACT(Sb[:, h, :], dst, AF.Copy, [Tdst], [T_Sb[h]])

    def stage_B2b(S_, h):
        P.tag = "stage_B2b"
        A, TA = S_.A, S_.TA
        for s in range(NS):
            ACT(S_.Sin[:, s, :], S_.Sin[:, s, :], AF.Identity, [S_.T_Sin, TA[1]], [S_.T_Sin], scale=A[1][:, NP + s:NP + s + 1])
        for g in range((NS + 3) // 4):
            bank = BK_D0 if g % 2 == 0 else BK_D1
            ss_ = list(range(g * 4, min(NS, g * 4 + 4)))
            for s in ss_:
                MM_(ps[:, bank, (s % 4) * 128:(s % 4 + 1) * 128], sel[:, s * 128:(s + 1) * 128], S_.i_s[:, :], True, True,
                    [T_const, S_.T_is], [T_ps[bank]], signal=(s == ss_[-1]))
            for s in ss_:
                STT(S_.Sin[:, s, :], ps[:, bank, (s % 4) * 128:(s % 4 + 1) * 128], A[7][:, NP + s:NP + s + 1], S_.Sin[:, s, :],
                    ALU.mult, ALU.add, [T_ps[bank], TA[7], S_.T_Sin], [S_.T_Sin])

    def stage_B3(S_, h, srow0):
        P.tag = "stage_B3"
        A, TA = S_.A, S_.TA
        for c in range(NCH):
            j = c // 2
            if c % 2 == 0:
                MM_(ps[:, BK_O, j * 128:(j + 1) * 128], S_.i_tm[:, j, :], S_.att_sb[:, j * 128:(j + 1) * 128], True, False,
                    [S_.T_itm[j], S_.T_att], [T_ps[BK_O]], False)
            if c == 0:
                sbc, Tsbc = Sbprev[:, h, :], T_Sbprev[h]
            else:
                sbc, Tsbc = S_.Sball[:, c - 1, :], S_.T_Sball[c - 1]
            MM_(ps[:, BK_O, c * 64:(c + 1) * 64], sbc, S_.qb[:, c * 64:(c + 1) * 64], False, c % 2 == 1,
                [Tsbc, S_.T_qb], [T_ps[BK_O]], signal=(c % 2 == 1))
        ACT(Sb[:, h, :], S_.Sball[:, NCH - 1, :], AF.Copy, [S_.T_Sball[NCH - 1]], [T_Sb[h]])
        for s in range(NS):
            MM_(ps[:, BK_ATT, s:s + 1], S_.Sin[:, s, :], A[0][:, NP + s:NP + s + 1], True, True,
                [S_.T_Sin, TA[0]], [T_ps[BK_ATT]], signal=(s == NS - 1))
        DMA("sp", ss_d[srow0:srow0 + NS, h].rearrange("s d e -> d s e"), S_.Sin[:], "sout", [S_.T_Sin], [])
        ACT(A[0][:, 0:NP], ps[:, BK_O, 0:NP], AF.Copy, [T_ps[BK_O], TA[0]], [TA[0]])
        ACT(A[0][:, NP:NT], ps[:, BK_ATT, 0:NS], AF.Copy, [T_ps[BK_ATT], TA[0]], [TA[0]])
        ACT(A[2], A[0], AF.Square, [TA[0]], [TA[2]])

    def stage_N(S_, h):
        P.tag = "stage_N"
        A, TA = S_.A, S_.TA
        bb = ps_fm()
        for hf in range(2):
            MM_(ps[:, bb + hf, 0:HW], onesf[:], A[2][:, hf * HW:(hf + 1) * HW], True, True,
                [T_const, TA[2]], [T_ps[bb + hf]], True)
        ACT(fm_view(A[3]), ps[:, bb:bb + 2, 0:HW], AF.Identity, [T_ps[bb], T_ps[bb + 1]], [TA[3]],
            scale=1.0 / 128, bias=epsb[:, 0:1])
        ACT(A[3], A[3], AF.Ln, [TA[3]], [TA[3]])
        ACT(A[3], A[3], AF.Exp, [TA[3]], [TA[3]], scale=-0.5)
        TT(A[6], A[0], A[3], ALU.mult, [TA[0], TA[3]], [TA[6]])
        STT(yb[:, h, :], A[6], gonT[:, h:h + 1], A[4], ALU.mult, ALU.mult, [TA[6], TA[4], T_const], [T_yb[h]])


    def prep_wm_load(g):
        P.tag = "setup"
        DMA("sp", wsld2[:], ws_d[g], "c1", [], [T_wsld2])

    def prep_wm(g):
        P.tag = "setup"
        TR(ps[:, BK_O, 0:128], wsld2[:], identf[:], [T_wsld2, T_const], [T_ps[BK_O]])
        TT(wmT[:, g, :], ps[:, BK_O, 0:128], trif[:], ALU.mult, [T_ps[BK_O], T_const], [T_wm[g]])

    def phase_prefix(first=False):
        stage_P1a(BS[0], 0, 0, True)
        stage_P2(BS[0], 0, True)
        proj_i_tr(BS[0], False)
        for h in range(H):
            S_ = BS[h % 2]
            Sn = BS[(h + 1) % 2]
            if h + 1 < H:
                stage_P1a(Sn, h + 1, 0, True)
            stage_TR(S_, h)
            if h + 1 < H:
                stage_P2(Sn, h + 1, True)
                proj_i_tr(Sn, False)
            stage_B2(S_, h, True)

    def phase_B(srow0):
        stage_P1a(BS[0], 0, srow0, False)
        stage_P1b(BS[0], 0, False)
        proj_i_tr(BS[0], True)
        stage_P2(BS[0], 0, False)
        for h in range(H):
            S_ = BS[h % 2]
            Sn = BS[(h + 1) % 2]
            stage_B2b(S_, h)
            if h + 1 < H:
                stage_P1a(Sn, h + 1, srow0, False)
            if h > 0:
                stage_N(Sn, h - 1)
            stage_TR(S_, h)
            if h + 1 < H:
                stage_P1b(Sn, h + 1, False)
                proj_i_tr(Sn, True)
            stage_B2(S_, h, False)
            if h + 1 < H:
                stage_P2(Sn, h + 1, False)
            stage_B3(S_, h, srow0)
        stage_N(BS[(H - 1) % 2], H - 1)

    def gelu_from(out, src, rd, wr, t1, Tt1, t2, Tt2):
        ACT(out, src, AF.Gelu_apprx_tanh, rd, wr)

    def vproj_prompt(Wap, Wt, blk, j):
        P.tag = "phase_A"
        b = ps_tm()
        for kc in range(KC):
            MM_(ps[:, b, 0:256], xT[:, kc, j * 128:(j + 1) * 128], Wap[:, kc, 0:256], kc == 0, kc == KC - 1,
                [Wt[0], Wt[1], T_xT[kc][j]], [T_ps[b]], signal=(kc == KC - 1))
        ACT(g_tm[:, j, blk * 256:(blk + 1) * 256], ps[:, b, 0:256], AF.Gelu_apprx_tanh, [T_ps[b]], [T_g[j]])

    def phase_A(srow0, last, vpre=()):
        P.tag = "phase_A"
        DMA("sp", lng_b[:], lng_d.partition_broadcast(128), "c0", [], [T_lnc])
        DMA("sp", lnb_b[:], lnb_d.partition_broadcast(128), "c0", [], [T_lnc])
        DMA("sp", bsb[:], bs_d.rearrange("g t -> (g t)").partition_broadcast(128), "c0", [], [T_bsb])
        def vproj(Wap, Wt, blk, j):
            rows = 128 if j < NB else NS
            col0 = j * 128 if j < NB else NP
            b = ps_tm()
            for kc in range(KC):
                MM_(ps[0:rows, b, 0:256], xT[:, kc, col0:col0 + rows], Wap[:, kc, 0:256], kc == 0, kc == KC - 1,
                    [Wt[0], Wt[1], T_xT[kc][j]], [T_ps[b]], signal=(kc == KC - 1))
            dst = g_tm[:, j, blk * 256:(blk + 1) * 256] if j < NB else gs[:, blk * 256:(blk + 1) * 256]
            Td = T_g[j] if j < NB else T_gs
            ACT(dst, ps[0:rows, b, 0:256], AF.Gelu_apprx_tanh, [T_ps[b]], [Td])

        nst = (E + 511) // 512
        ln_copies = []

        def ln_tile(j):
            rows = 128 if j < NB else NS
            G = g_tm[:, j, :] if j < NB else gs[:, :]
            Td = T_g[j] if j < NB else T_gs
            c0 = 8 + 32 * (j % 2)
            assert 6 * nst <= 24
            Tsm = T_smx[j % 2]
            st6 = small[0:rows, c0:c0 + 6 * nst]
            for k in range(nst):
                w = min(512, E - k * 512)
                P.op("dve", lambda k=k, w=w, G=G, rows=rows, c0=c0: dve_e.bn_stats(
                    out=small[0:rows, c0 + 6 * k: c0 + 6 + 6 * k], in_=G[:, k * 512: k * 512 + w]), [Td], [Tsm])
            mv = small[0:rows, 64 + 4 * (j % 2): 66 + 4 * (j % 2)]
            P.op("dve", lambda mv=mv, st6=st6: dve_e.bn_aggr(out=mv, in_=st6), [Tsm], [Tsm])
            mean = small[0:rows, 64 + 4 * (j % 2): 65 + 4 * (j % 2)]
            rs = small[0:rows, 65 + 4 * (j % 2): 66 + 4 * (j % 2)]
            nmr = small[0:rows, 66 + 4 * (j % 2): 67 + 4 * (j % 2)]
            TS(rs, rs, 1.0, EPS, ALU.mult, ALU.add, [Tsm], [Tsm], eng="pool")
            P.op("pool", lambda rs=rs, rows=rows: pool_e.tensor_tensor(out=rs, in0=rs, in1=mhalf[0:rows, 0:1], op=ALU.pow),
                 [Tsm, T_const], [Tsm])
            P.op("pool", lambda nmr=nmr, mean=mean, rs=rs: pool_e.tensor_tensor(out=nmr, in0=mean, in1=rs, op=ALU.mult),
                 [Tsm], [Tsm])
            TS(nmr, nmr, -1.0, 0.0, ALU.mult, ALU.add, [Tsm], [Tsm], eng="pool")
            TS(G, G, rs, nmr, ALU.mult, ALU.add, [Td, Tsm], [Td])
            TT(G, G, lng_b[0:rows, :], ALU.mult, [Td, T_lnc], [Td], eng="pool")
            TT(G, G, lnb_b[0:rows, :], ALU.add, [Td, T_lnc], [Td])
            if j == NB:
                DMA("sp", vs_d[srow0:srow0 + NS, :], G, "vout", [Td], [])
            elif last and j == NB - 1:
                DMA("sp", vp_d[:, :], G, "vout", [Td], [])
            ln_copies.append((j, G, Td))


        ln_done = set()

        for blk in range(E // 256):
            if blk < len(vpre):
                Wap, Wt, wi = vpre[blk]
                vproj(Wap, Wt, blk, NB)
                wst.done(wi)
                continue
            Wap, Wt, wi = wst.get([w_in_d[:, C.cV + blk * 256: C.cV + (blk + 1) * 256]], KC)
            for j in range(NB + 1):
                vproj(Wap, Wt, blk, j)
                if blk == E // 256 - 1:
                    ln_tile(j)
                    ln_done.add(j)
            wst.done(wi)
        for j in range(NB + 1):
            if j not in ln_done:
                ln_tile(j)
        def emit_ln_copies():
            for (j, G, Td) in ln_copies:
                if j == NB:
                    ACT(vns_bf[:, :], G, AF.Copy, [Td], [T_vns])
                else:
                    ACT(vn_bf[:, j, :], G, AF.Copy, [Td], [T_vn[j]])

        for gp in range(H // 2):
            W, Wt, wi = wst.get([w_in_d[:, C.cU + gp * 256: C.cU + (gp + 1) * 256]], KC)
            for gg in range(2):
                bU = fm_matmul(W, Wt, gg * 128, xT, xT_tiles, KC)
                ACT(fm_view(AA[2 + gg]), ps[:, bU:bU + 2, 0:HW], AF.Gelu_apprx_tanh, [T_ps[bU], T_ps[bU + 1]], [TAA[2 + gg]])
            wst.done(wi)
            W, Wt, wi = wst.get([w_in_d[:, C.cZA + gp * 256: C.cZA + (gp + 1) * 256]], KC)
            for gg in range(2):
                bZ = fm_matmul(W, Wt, gg * 128, xT, xT_tiles, KC)
                ACT(fm_view(AA[4 + gg]), ps[:, bZ:bZ + 2, 0:HW], AF.Silu, [T_ps[bZ], T_ps[bZ + 1]], [TAA[4 + gg]])
            wst.done(wi)
            if gp == 0:
                emit_ln_copies()
            for gg in range(2):
                g = gp * 2 + gg
                for j in range(NB):
                    MM_(ps[:, 6, j * 128:(j + 1) * 128], vn_bf[:, j, g * 128:(g + 1) * 128], wmT[:, g, :], True, True,
                        [T_vn[j], T_wm[g]], [T_ps[6]], signal=(j == NB - 1))
                q = ps_q7()
                MM_(ps[:, 4, q * 128: q * 128 + NS], vns_bf[:, g * 128:(g + 1) * 128], Rsm[:, g, :], True, True,
                    [T_vns, T_const], [T_q7[q]], True)
                msb = AA[gg]
                TT(msb[:, 0:NP].rearrange("p (j t) -> p j t", t=128), ps[:, 6, 0:NP].rearrange("p (j t) -> p j t", t=128),
                   bsb[:, g, :].unsqueeze(1).broadcast_to([128, NB, 128]), ALU.add, [T_ps[6], T_bsb], [TAA[gg]])
                TS(msb[:, NP:NT], ps[:, 4, q * 128: q * 128 + NS], bs0[:, g:g + 1], None, ALU.add, None,
                   [T_q7[q], T_const, TAA[gg]], [TAA[gg]])
                TT(AA[2 + gg], AA[2 + gg], AA[gg], ALU.mult, [TAA[2 + gg], TAA[gg]], [TAA[2 + gg]])
                TT(ya[:, g, :], AA[2 + gg], AA[4 + gg], ALU.mult, [TAA[2 + gg], TAA[4 + gg]], [T_ya[g]])

    def phase_M():
        P.tag = "phase_M"
        for cp in range(D // 256):
            W, Wt, wi = wst.get([w_in_d[:, C.cGA + cp * 256: C.cGA + (cp + 1) * 256]], KC)
            for cc in range(2):
                b = fm_matmul(W, Wt, cc * 128, xT, xT_tiles, KC)
                ACT(fm_view(MM[cc]), ps[:, b:b + 2, 0:HW], AF.Sigmoid, [T_ps[b], T_ps[b + 1]], [TM[cc]])
            wst.done(wi)
            W, Wt, wi = wst.get([w_pa_d[:, cp * 256:(cp + 1) * 256]], H)
            for cc in range(2):
                b = fm_matmul(W, Wt, cc * 128, ya, lambda kc: [T_ya[kc]], H)
                TT(fm_view(MM[cc]), ps[:, b:b + 2, 0:HW], fm_view(MM[cc]), ALU.mult, [T_ps[b], T_ps[b + 1], TM[cc]], [TM[cc]])
            wst.done(wi)
            W, Wt, wi = wst.get([w_in_d[:, C.cGB + cp * 256: C.cGB + (cp + 1) * 256]], KC)
            for cc in range(2):
                b = fm_matmul(W, Wt, cc * 128, xT, xT_tiles, KC)
                ACT(fm_view(MM[2 + cc]), ps[:, b:b + 2, 0:HW], AF.Sigmoid, [T_ps[b], T_ps[b + 1]], [TM[2 + cc]])
            wst.done(wi)
            W, Wt, wi = wst.get([w_pb_d[:, cp * 256:(cp + 1) * 256]], H)
            for cc in range(2):
                c = cp * 2 + cc
                b = fm_matmul(W, Wt, cc * 128, yb, lambda kc: [T_yb[kc]], H)
                TT(fm_view(MM[2 + cc]), ps[:, b:b + 2, 0:HW], fm_view(MM[2 + cc]), ALU.mult,
                   [T_ps[b], T_ps[b + 1], TM[2 + cc]], [TM[2 + cc]])
                TT(merged[:, c, :], MM[cc], MM[2 + cc], ALU.add, [TM[cc], TM[2 + cc]], [T_mg[c]])
            wst.done(wi)

    def phase_O(row0, srow0):
        P.tag = "phase_O"
        DMA("sp", gpost_b[:], gpost_d.partition_broadcast(128), "c0", [], [T_gpost])
        nblk = D // 256
        for blk in range(nblk):
            Wap, Wt, wi = wst.get([w_o_d[:, blk * 256:(blk + 1) * 256]], KC)
            for j in range(NB + 1):
                rows = 128 if j < NB else NS
                col0 = j * 128 if j < NB else NP
                b = ps_tm()
                for kc in range(KC):
                    MM_(ps[0:rows, b, 0:256], merged[:, kc, col0:col0 + rows], Wap[:, kc, 0:256], kc == 0, kc == KC - 1,
                        [Wt[0], Wt[1], T_mg[kc]], [T_ps[b]], signal=(kc == KC - 1))
                dst = y_tm[:, j, blk * 256:(blk + 1) * 256] if j < NB else ys_tm[:, blk * 256:(blk + 1) * 256]
                Td = T_y[j] if j < NB else T_ys
                ACT(dst, ps[0:rows, b, 0:256], AF.Copy, [T_ps[b]], [Td])
                ACT(junkO[0:rows, :], ps[0:rows, b, 0:256], AF.Square, [T_ps[b]], [T_junkO, T_ssq[j]],
                    accum_out=ssq[0:rows, j * nblk + blk: j * nblk + blk + 1])
            wst.done(wi)
        ACT(small[:, 4:5], small[:, 5:6], AF.Copy, [], T_ssq + [T_small])
        switch("r3", T_xf)
        order = [NB] + list(range(NB))
        for n, j in enumerate(order):
            rows = 128 if j < NB else NS
            Y = y_tm[:, j, :] if j < NB else ys_tm[:, :]
            Td = T_y[j] if j < NB else T_ys
            if j < NB:
                xsrc = xp_d[row0 + j * 128: row0 + (j + 1) * 128, :]
                dstd = yp_d[row0 + j * 128: row0 + (j + 1) * 128, :]
            else:
                xsrc = xs_d[srow0:srow0 + NS, :]
                dstd = ys_d[srow0:srow0 + NS, :]
            xb, Txb = xf[n % 2], T_xf[n % 2]
            DMA("sp", xb[0:rows, :], xsrc, "xq%d" % (n % 2), [], [Txb])
            sc = small[0:rows, 96 + (n % 2): 97 + (n % 2)]
            Tsm = T_smx[n % 2]
            P.op("dve", lambda j=j, rows=rows, sc=sc: dve_e.tensor_reduce(
                out=sc, in_=ssq[0:rows, j * nblk:(j + 1) * nblk], axis=mybir.AxisListType.X, op=ALU.add),
                [T_ssq[j]], [Tsm])
            TS(sc, sc, 1.0 / D, EPS, ALU.mult, ALU.add, [Tsm], [Tsm], eng="pool")
            P.op("pool", lambda sc=sc, rows=rows: pool_e.tensor_tensor(out=sc, in0=sc, in1=mhalf[0:rows, 0:1], op=ALU.pow),
                 [Tsm, T_const], [Tsm])
            STT(Y, Y, sc, gpost_b[0:rows, :], ALU.mult, ALU.mult, [Td, Tsm, T_gpost], [Td])
            hD = D // 2
            for hh in range(2):
                cs = slice(hh * hD, (hh + 1) * hD)
                Tq = T_yq[j][hh]
                TT(Y[:, cs], Y[:, cs], xb[0:rows, cs], ALU.add, [Td, Txb], [Tq])
                DMA("sp", dstd[:, cs], Y[:, cs], "yout%d" % ((2 * n + hh) % 4), [Tq], [])

    def all_xT():
        return [t for row in T_xT for t in row]

    def all_yq():
        return [t for row in T_yq for t in row]

    def schedule():
        for k in rot:
            rot[k] = 0
        rot["nfm"], rot["ntm"] = 2, 4
        wst.reset()
        region["r2"] = [T_wsld]
        region["r3"] = []
        setup()
        for p in range(NPASS):
            switch("r2", T_xs[0:1] + [T_junk])
            switch("r3", T_xs[1:3])
            phase_X(xpre_d, p * NP, 0, False)
            switch("r2", BS[0].tiles)
            switch("r3", BS[1].tiles)
            phase_prefix(first=(p == 0))
            if getattr(C, "stop_after", None) == "prefix0":
                return
        if getattr(C, "stop_after", None) == "prefix":
            return
        for p in range(NPASS):
            switch("r2", T_xs[0:1] + [T_junk])
            switch("r3", T_xs[1:3])
            nvpre = min(2, E // 256)
            vpre = [wst.get([w_in_d[:, C.cV + blk * 256: C.cV + (blk + 1) * 256]], KC) for blk in range(nvpre)]

            def vhook(j, vpre=vpre):
                for blk, (Wap, Wt, wi) in enumerate(vpre):
                    vproj_prompt(Wap, Wt, blk, j)

            phase_X(xp_d, p * NP, p * NS, True, vhook)
            switch("r2", A_tiles)
            switch("r3", A3_tiles)
            phase_A(p * NS, p == NPASS - 1, vpre)
            switch("r2", BS[0].tiles)
            switch("r3", BS[1].tiles)
            phase_B(p * NS)
            if getattr(C, "stop_after", None) == "B%d" % p:
                return
            if p == NPASS - 1:
                DMA("sp", sp_d.rearrange("h d e -> d h e"), Sf[:], "spout", T_Sf, [])
            switch("r2", TM)
            switch("r3", T_mg)
            rot["nfm"] = 4
            phase_M()
            rot["nfm"], rot["ntm"] = 2, 8
            if getattr(C, "stop_after", None) == "M%d" % p:
                return
            switch("r2", O_tiles)
            P.inherit(T_y, all_xT() + T_ya + T_yb + T_g)
            phase_O(p * NP, p * NS)
            rot["nfm"], rot["ntm"] = 2, 4
            P.inherit(all_xT() + T_ya + T_yb + T_g, T_y + all_yq() + [T_ys])

    P.dry = True
    schedule()
    P.dry = False
    schedule()

    sem_names = ["s_" + e for e in Prog.ENG] + ["l_" + k for k in P.lanes]
    sems = {n: es.enter_context(nc.semaphore(n)) for n in sem_names}
    for lane, n in P.lanes.items():
        P.ops["sp"].append(("wait", "l_" + lane, 16 * n))
    for e in ["pe", "act", "dve", "pool"]:
        if P.cnt[e] > 0:
            P.ops["sp"].append(("wait", "s_" + e, P.cnt[e]))

    eng_map = {"pe": pe_e, "act": act_e, "dve": dve_e, "pool": pool_e, "sp": sp_e}

    def run_engine(name):
        e = eng_map[name]
        for item in P.ops[name]:
            if item[0] == "wait":
                e.wait_ge(sems[item[1]], item[2])
            elif item[0] == "op":
                inst = item[1]()
                if item[2] is not None:
                    inst.then_inc(sems[item[2]], 1)
            else:
                inst = item[1]()
                inst.then_inc(sems[item[2]], 16)

    with nc.allow_non_contiguous_dma(reason="small parameter vectors / state layouts"):
        with nc.Block() as block:
            @block.tensor
            def _(t):
                run_engine("pe")

            @block.scalar
            def _(t):
                run_engine("act")

            @block.vector
            def _(t):
                run_engine("dve")

            @block.gpsimd
            def _(t):
                run_engine("pool")

            @block.sync
            def _(t):
                run_engine("sp")
    es.close()
    return nc, P


def _consts(cfg):
    NP, NS = cfg.NP, cfg.NS
    ident = np.eye(128, dtype=np.float32)
    s = np.arange(128)[:, None]
    t = np.arange(128)[None, :]
    tri = (s <= t).astype(np.float32)
    mask01 = np.ones((128, NP), np.float32)
    mask01[:, ::64] = 0.0
    blk = ((s // 64 == t // 64) & (s <= t)).astype(np.float32)
    maskT = np.tile(blk, (1, NP // 128))
    sel = np.zeros((NS, NS, 128), np.float32)
    for j in range(NS):
        sel[j, j, :] = 1.0
    return dict(c_ident=ident, c_tri=tri, c_mask01=mask01, c_maskT=maskT, c_sel=sel.reshape(NS, NS * 128))


_CACHE = {}


def run(cfg, x_prompt, x_sample, state_hgrn, lb_logits, g_pre, w_in, ln_g, ln_b, w_s, b_s,
        g_onorm, w_pa, w_pb, w_o, g_post, trace=False):
    C = cfg
    f = lambda a: np.ascontiguousarray(np.asarray(a, dtype=np.float32))
    x_prompt, x_sample, state_hgrn = f(x_prompt), f(x_sample), f(state_hgrn)
    B, SEQ, D = x_prompt.shape
    NTOK = C.NPASS * C.NP
    assert SEQ == 2 * NTOK and B * 2 == C.NCORES
    NSAMP = C.NPASS * C.NS
    DB = x_sample.shape[0]
    assert DB == NSAMP * C.NCORES
    key = (C.D, C.NP, C.NS, C.NPASS)
    if key not in _CACHE:
        _CACHE[key] = build_program(C)
    nc, _ = _CACHE[key]
    shared = dict(w_in=f(w_in)[0], w_pa=f(w_pa)[0], w_pb=f(w_pb)[0], w_o=f(w_o)[0], lb_logits=f(lb_logits),
                  g_pre=f(g_pre)[0], ln_g=f(ln_g)[0], ln_b=f(ln_b)[0], w_s=f(w_s)[0], b_s=f(b_s)[0],
                  g_onorm=f(g_onorm)[0], g_post=f(g_post)[0])
    shared.update(_consts(C))
    xs2d = x_sample.reshape(DB, D)
    in_maps = []
    for c in range(C.NCORES):
        b, half = c // 2, c % 2
        m = dict(shared)
        m["xp"] = x_prompt[b, half * NTOK:(half + 1) * NTOK]
        m["xpre"] = x_prompt[b, 0:NTOK] if half == 1 else np.zeros((NTOK, D), np.float32)
        m["xs"] = xs2d[c * NSAMP:(c + 1) * NSAMP]
        m["st"] = state_hgrn[0, c * NSAMP:(c + 1) * NSAMP]
        in_maps.append(m)
    res = run_bass_kernel_spmd(nc, in_maps, core_ids=list(range(C.NCORES)), trace=trace)
    R = res.results
    yp = np.stack([np.concatenate([R[2 * b]["yp"], R[2 * b + 1]["yp"]], axis=0) for b in range(B)])
    ys = np.concatenate([R[c]["ys"] for c in range(C.NCORES)], axis=0).reshape(DB, 1, D)
    sp = np.stack([R[2 * b + 1]["sp"] for b in range(B)])[None]
    ss = np.concatenate([R[c]["ss"] for c in range(C.NCORES)], axis=0)[None]
    vp = np.stack([R[2 * b + 1]["vp"] for b in range(B)])[None]
    vs = np.concatenate([R[c]["vs"] for c in range(C.NCORES)], axis=0).reshape(DB, 1, C.E)[None]
    out = (yp.astype(np.float32), ys.astype(np.float32), sp.astype(np.float32), ss.astype(np.float32),
           vp.astype(np.float32), vs.astype(np.float32))
    return out, res


def kernel(x_prompt, x_sample, state_hgrn, lb_logits, g_pre, w_in, ln_g, ln_b, w_s, b_s,
           g_onorm, w_pa, w_pb, w_o, g_post):
    out, _ = run(Cfg(), x_prompt, x_sample, state_hgrn, lb_logits, g_pre, w_in, ln_g, ln_b, w_s, b_s,
                 g_onorm, w_pa, w_pb, w_o, g_post)
    return out
```
